# Optimizing a Trainium2 kernel written in Bass

```python
import jax, jax.numpy as jnp
from jax import lax
import numpy as np

D_MODEL = 1024
BATCH = 8
SEQ = 2048
DEPTH = 2

POOL_WINDOWS = (2, 4, 8, 16)
POOL_GROUPS = len(POOL_WINDOWS)
POOL_GROUP_DIM = D_MODEL // POOL_GROUPS
POOL_WIDTH = POOL_GROUPS * POOL_GROUP_DIM
ATTN_GROUPS = ((128, 1), (512, 4), (2048, 16))
HEADS_PER_GROUP = 4
N_HEADS = HEADS_PER_GROUP * len(ATTN_GROUPS)
HEAD_DIM = D_MODEL // 8
ATTN_WIDTH = N_HEADS * HEAD_DIM
ATTN_OUT = HEADS_PER_GROUP * HEAD_DIM
Q_BLOCK = 64
NEG_INF = -1e30
N_BUCKETS = 32
MAX_DISTANCE = 1024
D_FF = ((8 * D_MODEL // 3 + 255) // 256) * 256
CONV_WIDTH = 3
EPS = 1e-6
SPLITS = [int(s) for s in np.cumsum([POOL_WIDTH, ATTN_WIDTH, ATTN_WIDTH, ATTN_WIDTH, D_MODEL])]
IN_WIDTH = POOL_WIDTH + 3 * ATTN_WIDTH + 2 * D_MODEL

kernel_name = "hybrid_pool_dilated_attn_convffn_encoder"


def _rmsnorm(x, g):
    xf = x.astype(jnp.float32)
    y = xf * lax.rsqrt(jnp.mean(xf * xf, axis=-1, keepdims=True) + EPS)
    return (y * g.astype(jnp.float32)).astype(x.dtype)


def _t5_buckets(rel):
    n = -rel
    half = N_BUCKETS // 2
    ret = (n < 0).astype(np.int32) * half
    n = np.abs(n)
    max_exact = half // 2
    large = max_exact + (np.log(np.maximum(n, 1) / max_exact)
                         / np.log(MAX_DISTANCE / max_exact) * (half - max_exact)).astype(np.int32)
    large = np.minimum(large, half - 1)
    return (ret + np.where(n < max_exact, n, large)).astype(np.int32)


def _centred_mean(u, window):
    S = u.shape[1]
    c = jnp.pad(jnp.cumsum(u.astype(jnp.float32), axis=1), ((0, 0), (1, 0), (0, 0)))
    i = np.arange(S)
    lo = np.clip(i - window // 2, 0, S)
    hi = np.clip(i - window // 2 + window, 0, S)
    count = (hi - lo).astype(np.float32)
    return ((c[:, hi] - c[:, lo]) / count[None, :, None]).astype(u.dtype)


def _dilated_group(q, k, v, bias, window, dilation):
    B, S, H, Dh = q.shape
    half = window // (2 * dilation)
    offs = dilation * np.arange(-half, half + 1)
    pad = half * dilation
    k_pad = jnp.pad(k, ((0, 0), (pad, pad), (0, 0), (0, 0)))
    v_pad = jnp.pad(v, ((0, 0), (pad, pad), (0, 0), (0, 0)))
    t = np.arange(Q_BLOCK)
    scale = HEAD_DIM ** -0.5

    def block(q0):
        pos = q0 + t[:, None] + offs[None, :]
        valid = (pos >= 0) & (pos < S)
        idx = pos + pad
        kb = jnp.take(k_pad, idx, axis=1)
        vb = jnp.take(v_pad, idx, axis=1)
        qb = lax.dynamic_slice_in_dim(q, q0, Q_BLOCK, axis=1)
        logits = jnp.einsum('bqhd,bqkhd->bhqk', qb, kb,
                            preferred_element_type=jnp.float32) * scale
        logits = logits + bias.astype(jnp.float32)[None, :, None, :]
        logits = jnp.where(valid[None, None], logits, NEG_INF)
        lse = jax.nn.logsumexp(logits, axis=-1)
        p = jnp.exp(logits - lse[..., None]).astype(vb.dtype)
        o = jnp.einsum('bhqk,bqkhd->bqhd', p, vb)
        return o, lse

    starts = jnp.arange(S // Q_BLOCK, dtype=jnp.int32) * Q_BLOCK
    o, lse = lax.map(block, starts)
    o = o.transpose(1, 0, 2, 3, 4).reshape(B, S, H, Dh)
    lse = lse.transpose(1, 0, 3, 2).reshape(B, S, H)
    return o, lse


def _token_mixer(xn, w_in, w_pool, pool_scale, w_a, w_b, w_o, rel_bias):
    B, S, _ = xn.shape
    u, q, k, v, g_pool, g_attn = jnp.split(xn @ w_in, SPLITS, axis=-1)

    ug = u.reshape(B, S, POOL_GROUPS, POOL_GROUP_DIM)
    pooled = jnp.stack([_centred_mean(ug[:, :, gi], w) for gi, w in enumerate(POOL_WINDOWS)], axis=2) - ug
    pool_out = jnp.einsum('bsgc,gcd->bsgd', pooled, w_pool).reshape(B, S, POOL_WIDTH) * pool_scale

    q = q.reshape(B, S, N_HEADS, HEAD_DIM)
    k = k.reshape(B, S, N_HEADS, HEAD_DIM)
    v = v.reshape(B, S, N_HEADS, HEAD_DIM)
    outs, lses = [], []
    for gi, (window, dilation) in enumerate(ATTN_GROUPS):
        hs = slice(gi * HEADS_PER_GROUP, (gi + 1) * HEADS_PER_GROUP)
        half = window // (2 * dilation)
        buckets = _t5_buckets(dilation * np.arange(-half, half + 1))
        bias = rel_bias[buckets][:, hs].T
        o_g, lse_g = _dilated_group(q[:, :, hs], k[:, :, hs], v[:, :, hs], bias, window, dilation)
        outs.append(o_g)
        lses.append(lse_g)
    wts = jax.nn.softmax(jnp.stack(lses, axis=0), axis=0)
    attn = jnp.einsum('gbsh,gbshd->bshd', wts, jnp.stack(outs, axis=0).astype(jnp.float32))
    attn_out = attn.astype(xn.dtype).reshape(B, S, ATTN_OUT)

    merged = jax.nn.sigmoid(g_pool) * (pool_out @ w_a) + jax.nn.sigmoid(g_attn) * (attn_out @ w_b)
    return merged @ w_o


def _conv_ffn(xn, w_up, conv_w, conv_b, w_down):
    a, gval = jnp.split(xn @ w_up, 2, axis=-1)
    a = lax.conv_general_dilated(a, conv_w[:, None, :], window_strides=(1,),
                                 padding=((CONV_WIDTH // 2, CONV_WIDTH // 2),),
                                 dimension_numbers=('NWC', 'WIO', 'NWC'),
                                 feature_group_count=D_FF) + conv_b
    return (jax.nn.gelu(a) * gval) @ w_down


def setup_inputs(seed: int = 0) -> dict:
    key = jax.random.key(seed)
    ks = jax.random.split(key, 16)
    f32 = jnp.float32

    def nrm(k, shape, scale):
        return jax.random.normal(k, shape, f32) * scale

    return {
        "x": nrm(ks[0], (BATCH, SEQ, D_MODEL), 1.0),
        "w_in": nrm(ks[1], (DEPTH, D_MODEL, IN_WIDTH), D_MODEL ** -0.5),
        "w_pool": nrm(ks[2], (DEPTH, POOL_GROUPS, POOL_GROUP_DIM, POOL_GROUP_DIM), POOL_GROUP_DIM ** -0.5),
        "pool_scale": 1.0 + nrm(ks[3], (DEPTH, POOL_WIDTH), 0.02),
        "w_a": nrm(ks[4], (DEPTH, POOL_WIDTH, D_MODEL), POOL_WIDTH ** -0.5),
        "w_b": nrm(ks[5], (DEPTH, ATTN_OUT, D_MODEL), ATTN_OUT ** -0.5),
        "w_o": nrm(ks[6], (DEPTH, D_MODEL, D_MODEL), D_MODEL ** -0.5),
        "norm1": 1.0 + nrm(ks[7], (DEPTH, D_MODEL), 0.02),
        "norm2": 1.0 + nrm(ks[8], (DEPTH, D_MODEL), 0.02),
        "w_up": nrm(ks[9], (DEPTH, D_MODEL, 2 * D_FF), D_MODEL ** -0.5),
        "conv_w": nrm(ks[10], (DEPTH, CONV_WIDTH, D_FF), CONV_WIDTH ** -0.5),
        "conv_b": nrm(ks[11], (DEPTH, D_FF), 0.01),
        "w_down": nrm(ks[12], (DEPTH, D_FF, D_MODEL), D_FF ** -0.5),
        "rel_bias": nrm(ks[13], (N_BUCKETS, N_HEADS), 0.2),
        "norm_f": 1.0 + nrm(ks[14], (D_MODEL,), 0.02),
    }


def reference(x, w_in, w_pool, pool_scale, w_a, w_b, w_o, norm1, norm2,
              w_up, conv_w, conv_b, w_down, rel_bias, norm_f):
    h = x
    for layer in range(DEPTH):
        h = h + _token_mixer(_rmsnorm(h, norm1[layer]), w_in[layer], w_pool[layer], pool_scale[layer],
                             w_a[layer], w_b[layer], w_o[layer], rel_bias)
        h = h + _conv_ffn(_rmsnorm(h, norm2[layer]), w_up[layer], conv_w[layer], conv_b[layer], w_down[layer])
    return _rmsnorm(h, norm_f)
```

```python
import numpy as np
from contextlib import ExitStack
import concourse.bass as bass
import concourse.mybir as mybir
from concourse.bass_utils import run_bass_kernel_spmd

F32 = mybir.dt.float32
BF16 = mybir.dt.bfloat16
AF = mybir.ActivationFunctionType
ALU = mybir.AluOpType

PE, ACT, DVE, POOL, SP = "pe", "act", "dve", "pool", "sp"
ENGINES = (PE, ACT, DVE, POOL, SP)

D_MODEL = 1024
SEQ = 2048
DEPTH = 2
NKC = 8
N_HEADS = 12
D_FF = 2816
NFC = 22
IN_WIDTH = 7680
OFF_Q, OFF_K, OFF_V, OFF_GP, OFF_GA = 1024, 2560, 4096, 5632, 6656
POOL_WINDOWS = (2, 4, 8, 16)
ATTN_GROUPS = ((128, 1), (512, 4), (2048, 16))
NEG_INF = -1e30
EPS = 1e-6
N_BUCKETS = 32
MAX_DISTANCE = 1024
VC_G1, VC_G2, VC_PS, VC_CW, VC_CB, VC_EPS, NV = 0, 16, 32, 48, 180, 224, 232
ARENA = 38720
NORM_OFF = 35648


class Sched:
    def __init__(self, same_engine_sync=(ACT, DVE, POOL)):
        self.ops = []
        self.same_engine_sync = set(same_engine_sync)

    tag = ""

    def op(self, eng, fn, reads=(), writes=(), dma=None, grp=None):
        self.ops.append(dict(eng=eng, fn=fn, reads=list(reads), writes=list(writes), dma=dma, tag=self.tag, grp=grp))

    @staticmethod
    def _overlap(a, b):
        if a is None or b is None:
            return True
        return not (a[1] <= b[0] or b[1] <= a[0])

    @staticmethod
    def _contains(a, b):
        if a is None:
            return True
        if b is None:
            return False
        return a[0] <= b[0] and b[1] <= a[1]

    def analyze(self):
        ops = self.ops
        live = {}
        deps = [set() for _ in ops]
        for i, o in enumerate(ops):
            eng = o["eng"]
            accs = [(r, "R") for r in o["reads"]] + [(w, "W") for w in o["writes"]]
            for (name, box), kind in accs:
                excl = name.startswith("ps:")
                lst = live.setdefault(name, [])
                for rbox, reng, ridx, rkind in lst:
                    if ridx == i:
                        continue
                    if excl:
                        deps[i].add(ridx)
                        continue
                    if kind == "R" and rkind == "R":
                        continue
                    if self._overlap(rbox, box):
                        deps[i].add(ridx)
            for (name, box), kind in accs:
                lst = live[name]
                excl = name.startswith("ps:")
                if kind == "W" or excl:
                    lst[:] = [r for r in lst if not self._contains(box, r[0])]
                    lst.append([box, eng, i, "W"])
                else:
                    rep = False
                    for r in lst:
                        if r[1] == eng and r[3] == "R" and r[0] == box:
                            r[2] = i
                            rep = True
                            break
                    if not rep:
                        lst.append([box, eng, i, "R"])
        self.deps = deps

    def _skip(self, oj, e):
        return oj["eng"] == e and oj["dma"] is None and e not in self.same_engine_sync

    def prepare(self):
        self.analyze()
        ops, deps = self.ops, self.deps
        needed = [False] * len(ops)
        for i, o in enumerate(ops):
            for j in deps[i]:
                if not self._skip(ops[j], o["eng"]):
                    needed[j] = True
        cnt = {e: 0 for e in ENGINES}
        dcnt = {}
        ms = [None] * len(ops)
        for i, o in enumerate(ops):
            if o["dma"] is not None:
                k = o["dma"]
                dcnt[k] = dcnt.get(k, 0) + 16
                ms[i] = ("d", k, dcnt[k])
            elif needed[i]:
                cnt[o["eng"]] += 1
                ms[i] = ("c", o["eng"], cnt[o["eng"]])
        gmax = {}
        for i, o in enumerate(ops):
            if o["dma"] is not None and o.get("grp") is not None:
                k = (o["dma"], o["grp"])
                gmax[k] = max(gmax.get(k, 0), ms[i][2])
        for i, o in enumerate(ops):
            if o["dma"] is not None and o.get("grp") is not None:
                ms[i] = ("d", o["dma"], gmax[(o["dma"], o["grp"])])
        self.ms = ms
        self.final_counts = (dict(cnt), dict(dcnt))
        self.streams = {e: [] for e in ENGINES}
        for i, o in enumerate(ops):
            self.streams[o["eng"]].append(i)
        return sorted(dcnt.keys())

    def emit_engine(self, e, eng, sems, dma_sems):
        ops, deps, ms = self.ops, self.deps, self.ms
        w = {}
        for i in self.streams[e]:
            o = ops[i]
            want = {}
            for j in deps[i]:
                if self._skip(ops[j], e):
                    continue
                m = ms[j]
                key = (m[0], m[1])
                if m[2] > want.get(key, 0):
                    want[key] = m[2]
            for key, val in want.items():
                if w.get(key, 0) >= val:
                    continue
                w[key] = val
                sem = dma_sems[key[1]] if key[0] == "d" else sems[key[1]]
                eng.wait_ge(sem, val)
            ins = o["fn"](eng)
            m = ms[i]
            if m is not None:
                if m[0] == "d":
                    ins.then_inc(dma_sems[m[1]], 16)
                else:
                    ins.then_inc(sems[m[1]], 1)


class Buf:
    def __init__(self, ap, shape, res, base=0, scale=1):
        self.ap, self.shape, self.res, self.base, self.scale = ap, tuple(shape), res, base, scale
        st, s = [], 1
        for d in reversed(self.shape):
            st.append(s)
            s *= d
        self.strides = tuple(reversed(st))
        self.n = s

    def r(self, *ranges):
        lo = hi = 0
        for k, d in enumerate(self.shape):
            a, b = ranges[k] if k < len(ranges) and ranges[k] is not None else (0, d)
            lo += a * self.strides[k]
            hi += (b - 1) * self.strides[k]
        return (self.res, (self.base + lo * self.scale, self.base + (hi + 1) * self.scale))

    def all(self):
        return (self.res, (self.base, self.base + self.n * self.scale))


def _t5_buckets(rel):
    n = -rel
    half = N_BUCKETS // 2
    ret = (n < 0).astype(np.int32) * half
    n = np.abs(n)
    max_exact = half // 2
    large = max_exact + (np.log(np.maximum(n, 1) / max_exact)
                         / np.log(MAX_DISTANCE / max_exact) * (half - max_exact)).astype(np.int32)
    large = np.minimum(large, half - 1)
    return (ret + np.where(n < max_exact, n, large)).astype(np.int32)


def _bias_tiles(rel_bias):
    out = np.full((128, 4, 640), NEG_INF, np.float32)
    i = np.arange(128)[:, None]
    for g, (window, dil) in enumerate(ATTN_GROUPS):
        half = window // (2 * dil)
        buckets = _t5_buckets(dil * np.arange(-half, half + 1))
        for h in range(4):
            col = rel_bias[:, g * 4 + h]
            def fill(dst, delta):
                valid = np.abs(delta) <= half
                j = np.clip(delta + half, 0, 2 * half)
                dst[...] = np.where(valid, col[buckets[j]], np.float32(NEG_INF))
            if g < 2:
                c = np.arange(128)[None, :]
                fill(out[:, h, g * 256:g * 256 + 128], i - c - 64)
                fill(out[:, h, g * 256 + 128:g * 256 + 256], i - c + 64)
            else:
                c = np.arange(32)[None, :]
                for m in range(4):
                    fill(out[:, h, 512 + 32 * m:512 + 32 * m + 32], i - 32 * m - c)
    return out.reshape(128, 2560)


def _band_mats():
    out = np.zeros((128, 20, 128), np.float32)
    for wi, w in enumerate(POOL_WINDOWS):
        def full(t_out0):
            M = np.zeros((3, 128, 128), np.float32)
            for c in range(128):
                t = t_out0 + c
                lo = max(t - w // 2, 0)
                hi = min(t - w // 2 + w, SEQ)
                cnt = hi - lo
                for ti in range(lo, hi):
                    d = ti // 128 - t_out0 // 128
                    M[d + 1, ti % 128, c] += 1.0 / cnt
                M[1, c, c] -= 1.0
            return M
        mi = full(128 * 5)
        out[:, wi * 5 + 0] = mi[0]
        out[:, wi * 5 + 1] = mi[1]
        out[:, wi * 5 + 2] = mi[2]
        out[:, wi * 5 + 3] = full(0)[1]
        out[:, wi * 5 + 4] = full(SEQ - 128)[1]
    return out.reshape(128, 2560)


def _chunked(v, nch):
    return np.ascontiguousarray(v.reshape(nch, 128).T)


def _pack_vecs(norm1, norm2, pool_scale, conv_w, conv_b):
    vecs = np.zeros((128, NV), np.float32)
    for l in range(DEPTH):
        vecs[:, VC_G1 + l * 8:VC_G1 + l * 8 + 8] = _chunked(norm1[l], 8)
        vecs[:, VC_G2 + l * 8:VC_G2 + l * 8 + 8] = _chunked(norm2[l], 8)
        vecs[:, VC_PS + l * 8:VC_PS + l * 8 + 8] = _chunked(pool_scale[l], 8)
        for j in range(3):
            o = VC_CW + (l * 3 + j) * NFC
            vecs[:, o:o + NFC] = _chunked(conv_w[l, j], NFC)
        vecs[:, VC_CB + l * NFC:VC_CB + l * NFC + NFC] = _chunked(conv_b[l], NFC)
    vecs[:, VC_EPS] = EPS
    return vecs


def build_nc(n_layers=DEPTH, final_norm=True, stop=None, dbg=False):
    nc = bass.Bass("TRN2", target_bir_lowering=False)
    dt_in = lambda n, s: nc.dram_tensor(n, s, F32, kind="ExternalInput").ap()
    x_d = dt_in("x", [SEQ, D_MODEL])
    w_in_d = dt_in("w_in", [DEPTH, D_MODEL, IN_WIDTH])
    w_pool_d = dt_in("w_pool", [DEPTH, 4, 256, 256])
    w_a_d = dt_in("w_a", [DEPTH, 1024, 1024])
    w_b_d = dt_in("w_b", [DEPTH, 512, 1024])
    w_o_d = dt_in("w_o", [DEPTH, 1024, 1024])
    w_up_d = dt_in("w_up", [DEPTH, 1024, 2 * D_FF])
    w_down_d = dt_in("w_down", [DEPTH, D_FF, 1024])
    vecs_d = dt_in("vecs", [128, NV])
    normf_d = dt_in("normf_bc", [128, 1024])
    ebraw_d = dt_in("ebraw", [128, 2560])
    bmat_d = dt_in("bmat", [128, 2560])
    y_d = nc.dram_tensor("y", [SEQ, D_MODEL], F32, kind="ExternalOutput").ap()
    dbg_d = {}
    if dbg:
        for l_ in range(n_layers):
            for nm in ("mix", "l"):
                dbg_d["%s%d" % (nm, l_)] = nc.dram_tensor("d_%s%d" % (nm, l_), [128, 8 * SEQ], F32, kind="ExternalOutput").ap()
            dbg_d["attn%d" % l_] = nc.dram_tensor("d_attn%d" % l_, [128, 4 * SEQ], F32, kind="ExternalOutput").ap()

    S = Sched()
    es = ExitStack()
    with es:
        sbt = lambda n, s, d: es.enter_context(nc.sbuf_tensor(n, s, d))
        hT_t = sbt("hT", [128, 8, SEQ], F32)
        xnT_t = sbt("xnT", [128, 8, SEQ], BF16)
        slots_t = [sbt("ws%d" % i, [128, 4096], BF16) for i in range(4)]
        arena_t = sbt("arena", [128, ARENA], BF16)
        ident_t = sbt("ident", [128, 128], F32)
        ones_t = sbt("onesb", [128, 128], BF16)
        invd_t = sbt("invdb", [128, 128], BF16)
        vecs_t = sbt("vecs_sb", [128, NV], F32)
        banks = [es.enter_context(nc.psum_tensor("pb%d" % i, [128, 512], F32)) for i in range(8)]

        hT = Buf(hT_t, (8, SEQ), "hT")
        xnT = Buf(xnT_t, (8, SEQ), "xnT")
        ident = Buf(ident_t, (128,), "ident")
        onesb = Buf(ones_t, (128,), "onesb")
        invdb = Buf(invd_t, (128,), "invdb")
        vecs = Buf(vecs_t, (NV,), "vecs")
        VECS_R = [vecs.all()]

        def vcol(c):
            return vecs_t[:, c:c + 1]

        class Arena:
            def __init__(self, start=0):
                self.off = start

            def take(self, shape, dt=BF16):
                n = int(np.prod(shape))
                ne = n * (2 if dt == F32 else 1)
                self.off = (self.off + 15) // 16 * 16
                assert self.off + ne <= ARENA, ("arena overflow", self.off, ne)
                v = arena_t[:, self.off:self.off + ne]
                if dt == F32:
                    v = v.bitcast(F32)
                if len(shape) == 2:
                    v = v.rearrange("p (a b) -> p a b", a=shape[0])
                elif len(shape) == 3:
                    v = v.rearrange("p (a b c) -> p a b c", a=shape[0], b=shape[1])
                b = Buf(v, shape, "arena", self.off, 2 if dt == F32 else 1)
                self.off += ne
                return b

        gb = [0]
        hb = [0]

        def gbank():
            gb[0] = (gb[0] + 1) % 4
            return gb[0]

        def hbank():
            hb[0] = (hb[0] + 1) % 3
            return 4 + hb[0]

        NORM_BANK = 7

        ab_ = [0]

        def abank():
            ab_[0] = (ab_[0] + 1) % 6
            return ab_[0]

        xb_ = [0]

        def xbank():
            xb_[0] = (xb_[0] + 1) % 7
            return xb_[0]

        def PS(b):
            return ("ps:%d" % b, None)

        def mm(out, lhsT, rhs, start, stop, reads, writes, skip=False):
            kw = dict(skip_group_check=True) if skip else {}
            S.op(PE, lambda e: e.matmul(out, lhsT=lhsT, rhs=rhs, start=start, stop=stop, **kw), reads, writes)

        def tr(out, in_, reads, writes):
            S.op(PE, lambda e: e.transpose(out=out, in_=in_, identity=ident_t[:]), reads + [ident.all()], writes)

        def act(out, in_, func, reads, writes, **kw):
            S.op(ACT, lambda e: e.activation(out=out, in_=in_, func=func, **kw), reads, writes)

        def dcopy(out, in_, reads, writes):
            S.op(DVE, lambda e: e.tensor_copy(out=out, in_=in_), reads, writes)

        tog = [0]

        def evac(out, in_, reads, writes):
            tog[0] ^= 1
            if tog[0]:
                act(out, in_, AF.Copy, reads, writes)
            else:
                dcopy(out, in_, reads, writes)

        def dtt(out, in0, in1, op, reads, writes):
            S.op(DVE, lambda e: e.tensor_tensor(out=out, in0=in0, in1=in1, op=op), reads, writes)

        def dstt(out, in0, scalar, in1, op0, op1, reads, writes):
            S.op(DVE, lambda e: e.scalar_tensor_tensor(out=out, in0=in0, scalar=scalar, in1=in1, op0=op0, op1=op1),
                 reads, writes)

        def drecip(out, in_, reads, writes):
            S.op(DVE, lambda e: e.reciprocal(out=out, in_=in_), reads, writes)

        def ptt(out, in0, in1, op, reads, writes):
            S.op(POOL, lambda e: e.tensor_tensor(out=out, in0=in0, in1=in1, op=op), reads, writes)

        def pts(out, in0, s1, s2, op0, op1, reads, writes):
            S.op(POOL, lambda e: e.tensor_scalar(out=out, in0=in0, scalar1=s1, scalar2=s2, op0=op0, op1=op1),
                 reads, writes)

        def dts(out, in0, s1, s2, op0, op1, reads, writes):
            S.op(DVE, lambda e: e.tensor_scalar(out=out, in0=in0, scalar1=s1, scalar2=s2, op0=op0, op1=op1),
                 reads, writes)

        def dma(q, out, in_, reads, writes, key, grp=None):
            S.op(q, lambda e: e.dma_start(out=out, in_=in_), reads, writes, dma=key, grp=grp)

        slot_i = [0]

        def next_slot():
            slot_i[0] = (slot_i[0] + 1) % 4
            return slot_i[0]

        def SL(i):
            return ("ws%d" % i, None)

        fill_id = [0]

        def new_fill():
            fill_id[0] += 1

        def wload(si, dst, src, part=None):
            box = None if part is None else (part, part + 1)
            dma(POOL, dst, src, [], [("ws%d" % si, box)], "ws%d" % si, grp=(si, fill_id[0]))

        def slot_k8(si, ncols=512):
            return slots_t[si][:, 0:8 * ncols].rearrange("p (k n) -> p k n", k=8)

        S.op(DVE, lambda e: e.memset(ident_t[:], 1.0), [], [ident.all()])
        S.op(POOL, lambda e: e.affine_select(out=ident_t[:], in_=ident_t[:], pattern=[[-1, 128]],
                                               compare_op=ALU.is_equal, fill=0.0, base=0, channel_multiplier=1),
             [ident.all()], [ident.all()])
        S.op(DVE, lambda e: e.memset(ones_t[:], 1.0), [], [onesb.all()])
        S.op(DVE, lambda e: e.memset(invd_t[:], 1.0 / D_MODEL), [], [invdb.all()])
        dma(SP, vecs_t[:], vecs_d, [], VECS_R, "vecs")

        jobs = []

        def job(nslots, loads, compute, name=""):
            jobs.append((nslots, loads, compute, name))

        def input_compute(_):
            ar = Arena()
            xin = [ar.take((1024,), F32) for _ in range(16)]
            for t in range(16):
                dma(SP if t % 2 == 0 else ACT, xin[t].ap, x_d[128 * t:128 * t + 128, :], [], [xin[t].all()], "xin%d" % t)
            for t in range(16):
                xb = xin[t]
                for hf in range(2):
                    b = xbank()
                    for j in range(4):
                        c = hf * 4 + j
                        tr(banks[b][:, 128 * j:128 * j + 128], xb.ap[:, 128 * c:128 * c + 128], [xb.all()], [PS(b)])
                    evac(hT_t[:, 4 * hf:4 * hf + 4, 128 * t:128 * t + 128],
                         banks[b][:].rearrange("p (a b) -> p a b", a=4),
                         [PS(b)], [hT.r((4 * hf, 4 * hf + 4), (128 * t, 128 * t + 128))])
                if t % 4 == 3 and n_layers > 0:
                    run_tasks(norm_block_tasks(t // 4, VC_G1))

        job(0, None, input_compute, "input")

        _nar = Arena(NORM_OFF)
        n_sq = [_nar.take((512,), BF16) for _ in range(2)]
        n_rs = [_nar.take((512,), F32) for _ in range(2)]

        def norm_block_tasks(blk, gcol, out_t=None, out_buf=None):
            out_t = xnT_t if out_t is None else out_t
            out_buf = xnT if out_buf is None else out_buf
            t0 = 512 * blk
            b = NORM_BANK
            r = n_rs[blk % 2]

            def t_sq(c0):
                def pre():
                    for c in (c0, c0 + 1):
                        s_ = n_sq[c % 2]
                        act(s_.ap, hT_t[:, c, t0:t0 + 512], AF.Square, [hT.r((c, c + 1), (t0, t0 + 512))], [s_.all()])

                def post():
                    for c in (c0, c0 + 1):
                        s_ = n_sq[c % 2]
                        mm(banks[b][:, :], invd_t[:], s_.ap, c == 0, c == 7, [invdb.all(), s_.all()], [PS(b)])
                return (pre, post)

            def t_rstd():
                act(r.ap, banks[b][:, :], AF.Ln, [PS(b)] + VECS_R, [r.all()], bias=vcol(VC_EPS), scale=1.0)
                act(r.ap, r.ap, AF.Exp, [r.all()], [r.all()], scale=-0.5)

            def t_xn(c0):
                def f():
                    for c in (c0, c0 + 1):
                        dstt(out_t[:, c, t0:t0 + 512], hT_t[:, c, t0:t0 + 512], vcol(gcol + c), r.ap, ALU.mult, ALU.mult,
                             [hT.r((c, c + 1), (t0, t0 + 512)), r.all()] + VECS_R,
                             [out_buf.r((c, c + 1), (t0, t0 + 512))])
                return (f, None)

            return [t_sq(0), t_sq(2), t_sq(4), t_sq(6), (None, t_rstd), t_xn(0), t_xn(2), t_xn(4), t_xn(6)]

        def run_tasks(tasks):
            for (pre, post) in tasks:
                if pre is not None:
                    pre()
                if post is not None:
                    post()

        extras = {}

        def spread(tasks, j0, j1):
            n = j1 - j0
            for k, t in enumerate(tasks):
                extras.setdefault(j0 + (k * n) // len(tasks), []).append(t)

        def proj_F(dst_bank, wslot_ap, col0, src_t, src_buf, t0, nk=8, wres=None):
            for kc in range(nk):
                mm(banks[dst_bank][:, :], wslot_ap[:, kc, col0:col0 + 128], src_t[:, kc, t0:t0 + 512],
                   kc == 0, kc == nk - 1,
                   [wres, src_buf.r((kc, kc + 1), (t0, t0 + 512))], [PS(dst_bank)])

        def tiles_of(g):
            res = []
            if g == 0:
                for J in range(17):
                    b0, b1 = max(128 * J - 64, 0), min(128 * J + 64, SEQ)
                    res.append((J, b0 - (128 * J - 64), b1 - b0, b0, 1))
            elif g == 1:
                for r in range(4):
                    for J in range(5):
                        b0, b1 = max(128 * J - 64, 0), min(128 * J + 64, 512)
                        res.append((r * 5 + J, b0 - (128 * J - 64), b1 - b0, r + 4 * b0, 4))
            else:
                for r in range(16):
                    res.append((r, 0, 128, r, 16))
            return res

        def xnf32(c):
            return Buf(xnT_t[:, c, 0:1024].bitcast(F32), (512,), "xnT", base=c * SEQ, scale=2)

        ost_lo, ost_hi = [xnf32(0), xnf32(6)], [xnf32(1), xnf32(7)]
        nf_lo, nf_hi, sqs_lo, sqs_hi = xnf32(2), xnf32(3), xnf32(4), xnf32(5)
        rs2_t = sbt("rs2", [128, 4], F32)
        halo_t = sbt("halo", [128, NFC], F32)
        halo = Buf(halo_t, (NFC,), "halo")
        rs2 = Buf(rs2_t, (4,), "rs2")
        final_done = []

        def final_tile(t):
            if not final_done:
                dma(SP, nf_lo.ap, normf_d[:, 0:512], [], [nf_lo.all()], "nf0")
                dma(SP, nf_hi.ap, normf_d[:, 512:1024], [], [nf_hi.all()], "nf1")
            final_done.append(t)
            ol, oh = ost_lo[t % 2], ost_hi[t % 2]
            bl, bh = xbank(), xbank()
            for c in range(8):
                b = bl if c < 4 else bh
                j = c % 4
                tr(banks[b][:, 128 * j:128 * j + 128], hT_t[:, c, 128 * t:128 * t + 128],
                   [hT.r((c, c + 1), (128 * t, 128 * t + 128))], [PS(b)])
            if final_norm:
                act(sqs_lo.ap, banks[bl][:, :], AF.Square, [PS(bl)], [sqs_lo.all()])
                act(sqs_hi.ap, banks[bh][:, :], AF.Square, [PS(bh)], [sqs_hi.all()])
                S.op(DVE, lambda e: e.reduce_sum(out=rs2_t[:, 0:1], in_=sqs_lo.ap, axis=mybir.AxisListType.X),
                     [sqs_lo.all()], [rs2.r((0, 1))])
                S.op(DVE, lambda e: e.reduce_sum(out=rs2_t[:, 1:2], in_=sqs_hi.ap, axis=mybir.AxisListType.X),
                     [sqs_hi.all()], [rs2.r((1, 2))])
                dtt(rs2_t[:, 2:3], rs2_t[:, 0:1], rs2_t[:, 1:2], ALU.add, [rs2.r((0, 2))], [rs2.r((2, 3))])
                act(rs2_t[:, 3:4], rs2_t[:, 2:3], AF.Ln, [rs2.r((2, 3))] + VECS_R, [rs2.r((3, 4))],
                    bias=vcol(VC_EPS), scale=1.0 / D_MODEL)
                act(rs2_t[:, 3:4], rs2_t[:, 3:4], AF.Exp, [rs2.r((3, 4))], [rs2.r((3, 4))], scale=-0.5)
                dstt(ol.ap, banks[bl][:, :], rs2_t[:, 3:4], nf_lo.ap, ALU.mult, ALU.mult,
                     [PS(bl), rs2.r((3, 4)), nf_lo.all()], [ol.all()])
                dstt(oh.ap, banks[bh][:, :], rs2_t[:, 3:4], nf_hi.ap, ALU.mult, ALU.mult,
                     [PS(bh), rs2.r((3, 4)), nf_hi.all()], [oh.all()])
            else:
                dcopy(ol.ap, banks[bl][:, :], [PS(bl)], [ol.all()])
                dcopy(oh.ap, banks[bh][:, :], [PS(bh)], [oh.all()])
            dma(SP, y_d[128 * t:128 * t + 128, 0:512], ol.ap, [ol.all()], [], "outA%d" % (t % 2))
            dma(SP, y_d[128 * t:128 * t + 128, 512:1024], oh.ap, [oh.all()], [], "outB%d" % (t % 2))

        def add_layer(l):
            w_in_l = w_in_d[l].rearrange("(kc p) n -> p kc n", p=128)
            w_a_l = w_a_d[l].rearrange("(kc p) n -> p kc n", p=128)
            w_b_l = w_b_d[l].rearrange("(kc p) n -> p kc n", p=128)
            w_o_l = w_o_d[l].rearrange("(kc p) n -> p kc n", p=128)
            w_up_l = w_up_d[l].rearrange("(kc p) n -> p kc n", p=128)
            w_dn_l = w_down_d[l].rearrange("(kc p) n -> p kc n", p=128)

            ar = Arena()
            attnT = ar.take((4, SEQ))
            a_mark = ar.off
            ar_n1 = Arena(a_mark)
            ar = Arena(a_mark)
            kT = ar.take((3, SEQ))
            V0 = ar.take((17, 128))
            V1 = ar.take((20, 128))
            V2 = ar.take((16, 128))
            qTb = [ar.take((3, 512)) for _ in range(3)]
            ptr = [ar.take((512,)) for _ in range(5)]
            pt = [ar.take((512,)) for _ in range(5)]
            EB = ar.take((4, 640))
            ebr = ar.take((640,), F32)
            rden = ar.take((512,), F32)
            Vg = (V0, V1, V2)

            def norm1_compute(_):
                for h in range(4):
                    dma(SP, ebr.ap, ebraw_d[:, 640 * h:640 * h + 640], [], [ebr.all()], "ebr")
                    act(EB.ap[:, h, :], ebr.ap, AF.Exp, [ebr.all()], [EB.r((h, h + 1))])

            job(0, None, norm1_compute, "norm1")

            def kv_loads(slots, h):
                for si, off in ((slots[0], OFF_K), (slots[1], OFF_V)):
                    for g in range(3):
                        c0 = off + (g * 4 + h) * 128
                        wload(si, slot_k8(si)[:, :, 128 * g:128 * g + 128], w_in_l[:, :, c0:c0 + 128], part=g)

            def kv_compute(slots, h):
                sk_, sv_ = slots
                wk, wv = slot_k8(sk_), slot_k8(sv_)
                for g in range(3):
                    for blk in range(4):
                        b = xbank()
                        proj_F(b, wk, 128 * g, xnT_t, xnT, 512 * blk, wres=SL(sk_))
                        if g == 0:
                            evac(kT.ap[:, g, 512 * blk:512 * blk + 512], banks[b][:, :], [PS(b)],
                                 [kT.r((g, g + 1), (512 * blk, 512 * blk + 512))])
                        else:
                            d_ = 4 if g == 1 else 16
                            nb_ = 512 // d_
                            evac(kT.ap[:, g, :].rearrange("p (r b) -> p b r", r=d_)[:, nb_ * blk:nb_ * blk + nb_, :],
                                 banks[b][:, :].rearrange("p (b r) -> p b r", r=d_), [PS(b)], [kT.r((g, g + 1))])
                for g in range(3):
                    tl = tiles_of(g)
                    i = 0
                    while i < len(tl):
                        n = 1
                        if tl[i][2] == 128:
                            while (n < 4 and i + n < len(tl) and tl[i + n][2] == 128
                                   and tl[i + n][0] == tl[i][0] + n):
                                n += 1
                        b = xbank()
                        for q_ in range(n):
                            (ti, row0, cnt, tok0, step) = tl[i + q_]
                            for kc in range(8):
                                mm(banks[b][row0:row0 + cnt, 128 * q_:128 * q_ + 128],
                                   xnT_t[:, kc, tok0:tok0 + step * (cnt - 1) + 1:step],
                                   wv[:, kc, 128 * g:128 * g + 128], kc == 0, kc == 7,
                                   [SL(sv_), xnT.r((kc, kc + 1))], [PS(b)])
                        (ti, row0, cnt, tok0, step) = tl[i]
                        if n == 1:
                            evac(Vg[g].ap[row0:row0 + cnt, ti, :], banks[b][row0:row0 + cnt, 0:128], [PS(b)],
                                 [Vg[g].r((ti, ti + 1))])
                        else:
                            evac(Vg[g].ap[:, ti:ti + n, :],
                                 banks[b][:, 0:128 * n].rearrange("p (a b) -> p a b", a=n), [PS(b)],
                                 [Vg[g].r((ti, ti + n))])
                        i += n

            def q_loads(slots, h):
                si = slots[0]
                for g in range(3):
                    c0 = OFF_Q + (g * 4 + h) * 128
                    wload(si, slot_k8(si)[:, :, 128 * g:128 * g + 128], w_in_l[:, :, c0:c0 + 128], part=g)

            def q_compute(slots, h):
                sq_ = slots[0]
                wq = slot_k8(sq_)
                base_tag = S.tag

                def qproj(m):
                    S.tag = base_tag + ".qproj%d" % m
                    qb = qTb[m % 3]
                    for g in range(3):
                        b = gbank()
                        proj_F(b, wq, 128 * g, xnT_t, xnT, 512 * m, wres=SL(sq_))
                        if g == 0:
                            evac(qb.ap[:, g, :], banks[b][:, :], [PS(b)], [qb.r((g, g + 1))])
                        else:
                            d_ = 4 if g == 1 else 16
                            evac(qb.ap[:, g, :].rearrange("p (r c) -> p c r", r=d_),
                                 banks[b][:, :].rearrange("p (c r) -> p c r", r=d_), [PS(b)], [qb.r((g, g + 1))])

                def block_groups(m):
                    qb = qTb[m % 3]
                    nb, db = (4, 5) if m % 2 == 0 else (6, 7)
                    groups = []
                    for up in range(2):
                        items = []
                        for s_ in range(4):
                            J = 4 * m + s_ + up
                            b0, b1 = max(128 * J - 64, 0), min(128 * J + 64, SEQ)
                            row0, cnt = b0 - (128 * J - 64), b1 - b0
                            items.append((128 * s_, row0, cnt, kT.ap[:, 0, b0:b1], qb.ap[:, 0, 128 * s_:128 * s_ + 128],
                                          V0.ap[row0:row0 + cnt, J, :],
                                          banks[nb][:, 128 * s_:128 * s_ + 128], banks[db][:, 128 * s_:128 * s_ + 128]))
                        groups.append((items, (128 * up, 128 * up + 128), 4, 128))
                    for up in range(2):
                        items = []
                        for r in range(4):
                            J = m + up
                            b0, b1 = max(128 * J - 64, 0), min(128 * J + 64, 512)
                            row0, cnt = b0 - (128 * J - 64), b1 - b0
                            items.append((128 * r, row0, cnt,
                                          kT.ap[:, 1, 512 * r + b0:512 * r + b1],
                                          qb.ap[:, 1, 128 * r:128 * r + 128],
                                          V1.ap[row0:row0 + cnt, r * 5 + J, :],
                                          banks[nb][:, r:512:4], banks[db][:, r:512:4]))
                        groups.append((items, (256 + 128 * up, 256 + 128 * up + 128), 4, 128))
                    items = []
                    for r in range(16):
                        items.append((32 * r, 0, 128, kT.ap[:, 2, 128 * r:128 * r + 128], qb.ap[:, 2, 32 * r:32 * r + 32],
                                      V2.ap[:, r, :], banks[nb][:, r:512:16], banks[db][:, r:512:16]))
                    groups.append((items, (512 + 32 * m, 512 + 32 * m + 32), 16, 32))
                    return groups

                seq = []
                for m in range(4):
                    for gi, G in enumerate(block_groups(m)):
                        seq.append((m, gi, G))
                pidx = [0]

                def scores(n):
                    m, gi, (items, ebcols, nrep, ncol) = seq[n]
                    S.tag = base_tag + ".m%d.sc%d" % (m, gi)
                    qb = qTb[m % 3]
                    b = gbank()
                    for (c0, row0, cnt, lT, rh, vap, no, do) in items:
                        mm(banks[b][row0:row0 + cnt, c0:c0 + ncol], lT, rh, True, True,
                           [kT.all(), qb.all()], [PS(b)])
                    k = n % 5
                    act(ptr[k].ap, banks[b][:, :], AF.Exp, [PS(b)], [ptr[k].all()], scale=float(128 ** -0.5))
                    ebv = EB.ap[:, h, ebcols[0]:ebcols[1]].unsqueeze(1).broadcast_to([128, nrep, ncol])
                    dtt(pt[k].ap.rearrange("p (a b) -> p a b", a=nrep),
                        ptr[k].ap.rearrange("p (a b) -> p a b", a=nrep), ebv, ALU.mult,
                        [ptr[k].all(), EB.r((h, h + 1))], [pt[k].all()])

                def pv(n):
                    m, gi, (items, ebcols, nrep, ncol) = seq[n]
                    S.tag = base_tag + ".m%d.pv%d" % (m, gi)
                    nb, db = (4, 5) if m % 2 == 0 else (6, 7)
                    k = n % 5
                    first = (gi == 0)
                    for (c0, row0, cnt, lT, rh, vap, no, do) in items:
                        mm(no, vap, pt[k].ap[row0:row0 + cnt, c0:c0 + ncol], first, False,
                           [Vg[0].all(), Vg[1].all(), Vg[2].all(), pt[k].all()], [PS(nb)], skip=True)
                        mm(do, ones_t[row0:row0 + cnt, :], pt[k].ap[row0:row0 + cnt, c0:c0 + ncol], first, False,
                           [onesb.all(), pt[k].all()], [PS(db)], skip=True)
                        first = False
                    if gi == 4:
                        S.tag = base_tag + ".m%d.norm" % m
                        act(rden.ap, banks[db][:, :], AF.Ln, [PS(db)], [rden.all()])
                        act(rden.ap, rden.ap, AF.Exp, [rden.all()], [rden.all()], scale=-1.0)
                        dtt(attnT.ap[:, h, 512 * m:512 * m + 512], banks[nb][:, :], rden.ap, ALU.mult,
                            [PS(nb), rden.all()], [attnT.r((h, h + 1), (512 * m, 512 * m + 512))])

                LOOK = 4
                qproj(0)
                qproj(1)
                for n in range(LOOK):
                    scores(n)
                qproj(2)
                for n in range(len(seq)):
                    m, gi, _ = seq[n]
                    if gi == 1 and m == 1:
                        qproj(3)
                    pv(n)
                    if n + LOOK < len(seq):
                        scores(n + LOOK)

            for h in range(4):
                job(2, (lambda s, h=h: kv_loads(s, h)), (lambda s, h=h: kv_compute(s, h)), "kv%d" % h)
                job(1, (lambda s, h=h: q_loads(s, h)), (lambda s, h=h: q_compute(s, h)), "q%d" % h)

            ar = Arena(a_mark)
            bmat = ar.take((20, 128))
            u_sb = ar.take((9, 256))
            pooledT = ar.take((2, 1024))
            pool_outT = ar.take((8, 1024))
            merged = ar.take((8, 1024))
            sgp = [ar.take((512,)) for _ in range(2)]
            sga = [ar.take((512,)) for _ in range(2)]
            t1 = ar.take((512,), F32)
            t2 = ar.take((512,), F32)

            def pool_loads(slots, g):
                si = slots[0]
                wload(si, slots_t[si][:, 0:2048].rearrange("p (k n) -> p k n", k=8), w_in_l[:, :, 256 * g:256 * g + 256],
                      part=0)
                wload(si, slots_t[si][:, 2048:2560].rearrange("p (k n) -> p k n", k=2),
                      w_pool_d[l, g].rearrange("(kc p) n -> p kc n", p=128), part=1)

            def pool_compute(slots, hf, g):
                si = slots[0]
                wu = slots_t[si][:, 0:2048].rearrange("p (k n) -> p k n", k=8)
                wp = slots_t[si][:, 2048:2560].rearrange("p (k n) -> p k n", k=2)
                tl0 = max(8 * hf - 1, 0)
                tl1 = min(8 * hf + 9, 16)
                if hf == 0 and g == 0:
                    if dbg:
                        dma(POOL, dbg_d["attn%d" % l], attnT.ap.rearrange("p a b -> p (a b)"), [attnT.all()], [], "dbg")
                    dma(POOL, bmat.ap.rearrange("p a b -> p (a b)"), bmat_d, [], [bmat.all()], "bmat")
                for t in range(tl0, tl1):
                    b = xbank()
                    for kc in range(8):
                        mm(banks[b][:, 0:256], xnT_t[:, kc, 128 * t:128 * t + 128], wu[:, kc, :], kc == 0, kc == 7,
                           [SL(si), xnT.r((kc, kc + 1), (128 * t, 128 * t + 128))], [PS(b)])
                    evac(u_sb.ap[:, t - tl0, :], banks[b][:, 0:256], [PS(b)], [u_sb.r((t - tl0, t - tl0 + 1))])
                for cc in range(2):
                    for tb in range(2):
                        b = xbank()
                        for j in range(4):
                            t = 8 * hf + 4 * tb + j
                            terms = []
                            if t > 0:
                                terms.append((t - 1, 5 * g + 0))
                            terms.append((t, 5 * g + (3 if t == 0 else 4 if t == 15 else 1)))
                            if t < 15:
                                terms.append((t + 1, 5 * g + 2))
                            for k, (tin, bi) in enumerate(terms):
                                mm(banks[b][:, 128 * j:128 * j + 128],
                                   u_sb.ap[:, tin - tl0, 128 * cc:128 * cc + 128], bmat.ap[:, bi, :],
                                   k == 0, k == len(terms) - 1,
                                   [u_sb.r((tin - tl0, tin - tl0 + 1)), bmat.all()], [PS(b)])
                        evac(pooledT.ap[:, cc, 512 * tb:512 * tb + 512], banks[b][:, :], [PS(b)],
                             [pooledT.r((cc, cc + 1), (512 * tb, 512 * tb + 512))])
                for dc in range(2):
                    for tb in range(2):
                        b = xbank()
                        for kc in range(2):
                            mm(banks[b][:, :], wp[:, kc, 128 * dc:128 * dc + 128],
                               pooledT.ap[:, kc, 512 * tb:512 * tb + 512], kc == 0, kc == 1,
                               [SL(si), pooledT.r((kc, kc + 1), (512 * tb, 512 * tb + 512))], [PS(b)])
                        ch = 2 * g + dc
                        act(pool_outT.ap[:, ch, 512 * tb:512 * tb + 512], banks[b][:, :], AF.Identity,
                            [PS(b)] + VECS_R, [pool_outT.r((ch, ch + 1), (512 * tb, 512 * tb + 512))],
                            scale=vcol(VC_PS + 8 * l + ch))

            def mview(si, part):
                if part < 3:
                    return slots_t[si][:, 1024 * part:1024 * part + 1024].rearrange("p (k n) -> p k n", k=8)
                return slots_t[si][:, 3072:3584].rearrange("p (k n) -> p k n", k=4)

            def merge_loads(slots, mch):
                si = slots[0]
                wload(si, mview(si, 0), w_in_l[:, :, OFF_GP + 128 * mch:OFF_GP + 128 * mch + 128], part=0)
                wload(si, mview(si, 1), w_in_l[:, :, OFF_GA + 128 * mch:OFF_GA + 128 * mch + 128], part=1)
                wload(si, mview(si, 2), w_a_l[:, :, 128 * mch:128 * mch + 128], part=2)
                wload(si, mview(si, 3), w_b_l[:, :, 128 * mch:128 * mch + 128], part=3)

            def merge_compute(slots, hf, mch):
                si = slots[0]
                T0 = 1024 * hf
                wgp, wga, wa, wb = mview(si, 0), mview(si, 1), mview(si, 2), mview(si, 3)
                for tb in range(2):
                    t0 = T0 + 512 * tb
                    gp, ga = sgp[tb], sga[tb]
                    b = gbank()
                    proj_F(b, wgp, 0, xnT_t, xnT, t0, wres=SL(si))
                    act(gp.ap, banks[b][:, :], AF.Sigmoid, [PS(b)], [gp.all()])
                    b = gbank()
                    proj_F(b, wga, 0, xnT_t, xnT, t0, wres=SL(si))
                    act(ga.ap, banks[b][:, :], AF.Sigmoid, [PS(b)], [ga.all()])
                    b = hbank()
                    for kc in range(8):
                        mm(banks[b][:, :], wa[:, kc, :], pool_outT.ap[:, kc, 512 * tb:512 * tb + 512], kc == 0, kc == 7,
                           [SL(si), pool_outT.r((kc, kc + 1), (512 * tb, 512 * tb + 512))], [PS(b)])
                    dtt(t1.ap, banks[b][:, :], gp.ap, ALU.mult, [PS(b), gp.all()], [t1.all()])
                    b = hbank()
                    for kc in range(4):
                        mm(banks[b][:, :], wb[:, kc, :], attnT.ap[:, kc, t0:t0 + 512], kc == 0, kc == 3,
                           [SL(si), attnT.r((kc, kc + 1), (t0, t0 + 512))], [PS(b)])
                    dtt(t2.ap, banks[b][:, :], ga.ap, ALU.mult, [PS(b), ga.all()], [t2.all()])
                    ptt(merged.ap[:, mch, 512 * tb:512 * tb + 512], t1.ap, t2.ap, ALU.add,
                        [t1.all(), t2.all()], [merged.r((mch, mch + 1), (512 * tb, 512 * tb + 512))])

            def wo_loads(slots, mq):
                si = slots[0]
                wload(si, slot_k8(si), w_o_l[:, :, 512 * mq:512 * mq + 512])

            def wo_compute(slots, hf, mq):
                so = slots[0]
                T0 = 1024 * hf
                for mi in range(4):
                    mch = 4 * mq + mi
                    for tb in range(2):
                        t0 = T0 + 512 * tb
                        b = xbank()
                        for kc in range(8):
                            mm(banks[b][:, :], slot_k8(so)[:, kc, 128 * mi:128 * mi + 128],
                               merged.ap[:, kc, 512 * tb:512 * tb + 512], kc == 0, kc == 7,
                               [SL(so), merged.r((kc, kc + 1), (512 * tb, 512 * tb + 512))], [PS(b)])
                        dtt(hT_t[:, mch, t0:t0 + 512], banks[b][:, :], hT_t[:, mch, t0:t0 + 512], ALU.add,
                            [PS(b), hT.r((mch, mch + 1), (t0, t0 + 512))], [hT.r((mch, mch + 1), (t0, t0 + 512))])

            def wo1_loads(slots):
                for mq in range(2):
                    wload(slots[mq], slot_k8(slots[mq]), w_o_l[:, :, 512 * mq:512 * mq + 512])

            def wo1_compute(slots):
                tasks = norm_block_tasks(2, VC_G2 + 8 * l)
                for tb in range(2):
                    t0 = 1024 + 512 * tb
                    for mch in range(8):
                        so, mi = slots[mch // 4], mch % 4
                        tk = tasks[mch] if tb == 1 else (None, None)
                        if tk[0] is not None:
                            tk[0]()
                        b = xbank()
                        for kc in range(8):
                            mm(banks[b][:, :], slot_k8(so)[:, kc, 128 * mi:128 * mi + 128],
                               merged.ap[:, kc, 512 * tb:512 * tb + 512], kc == 0, kc == 7,
                               [SL(so), merged.r((kc, kc + 1), (512 * tb, 512 * tb + 512))], [PS(b)])
                        dtt(hT_t[:, mch, t0:t0 + 512], banks[b][:, :], hT_t[:, mch, t0:t0 + 512], ALU.add,
                            [PS(b), hT.r((mch, mch + 1), (t0, t0 + 512))], [hT.r((mch, mch + 1), (t0, t0 + 512))])
                        if tk[1] is not None:
                            tk[1]()
                run_tasks(tasks[8:])

            for hf in range(2):
                if hf == 1:
                    jp = len(jobs)
                    spread(norm_block_tasks(0, VC_G2 + 8 * l), jp, jp + 9)
                    spread(norm_block_tasks(1, VC_G2 + 8 * l), jp + 9, jp + 13)
                for g in range(4):
                    job(1, (lambda s, g=g: pool_loads(s, g)), (lambda s, hf=hf, g=g: pool_compute(s, hf, g)), "pool")
                for mch in range(8):
                    job(1, (lambda s, mch=mch: merge_loads(s, mch)), (lambda s, hf=hf, mch=mch: merge_compute(s, hf, mch)), "merge")
                if hf == 0:
                    for mq in range(2):
                        job(1, (lambda s, mq=mq: wo_loads(s, mq)), (lambda s, hf=hf, mq=mq: wo_compute(s, hf, mq)), "wo")
                else:
                    job(2, wo1_loads, wo1_compute, "wo1")

            def after_mix(_):
                if dbg:
                    dma(SP, dbg_d["mix%d" % l], hT_t[:].rearrange("p a b -> p (a b)"), [hT.all()], [], "dbg")

            job(0, None, after_mix)
            if stop == 'mix':
                return

            ar = Arena()
            actT = ar.take((NFC, 1024))
            a_sb = [ar.take((1040,), F32) for _ in range(2)]
            cv = [ar.take((1024,), F32) for _ in range(2)]
            gel = [ar.take((1024,)) for _ in range(2)]
            ar_n2 = ar
            cwc = lambda j, c: vcol(VC_CW + (l * 3 + j) * NFC + c)

            j_up0 = len(jobs)
            spread(norm_block_tasks(3, VC_G2 + 8 * l), j_up0, j_up0 + 14)

            def up_loads(slots, fq):
                nch = 4 if fq < 5 else 2
                sa_s, sg_s = slots
                wload(sa_s, slot_k8(sa_s)[:, :, 0:128 * nch], w_up_l[:, :, 512 * fq:512 * fq + 128 * nch])
                wload(sg_s, slot_k8(sg_s)[:, :, 0:128 * nch], w_up_l[:, :, D_FF + 512 * fq:D_FF + 512 * fq + 128 * nch])

            def up_compute(slots, hf, fq):
                nch = 4 if fq < 5 else 2
                sa_s, sg_s = slots
                T0 = 1024 * hf

                def part_a(ci):
                    fc = 4 * fq + ci
                    ab = a_sb[fc % 2]
                    cb_ = cv[fc % 2]
                    gl = gel[fc % 2]
                    for tb in range(2):
                        b = gbank()
                        proj_F(b, slot_k8(sa_s)[:, :, 128 * ci:128 * ci + 128], 0, xnT_t, xnT, T0 + 512 * tb, wres=SL(sa_s))
                        act(ab.ap[:, 8 + 512 * tb:8 + 512 * tb + 512], banks[b][:, :], AF.Copy, [PS(b)],
                            [ab.r((8 + 512 * tb, 8 + 512 * tb + 512))])
                def part_a2(ci):
                    fc = 4 * fq + ci
                    ab = a_sb[fc % 2]
                    cb_ = cv[fc % 2]
                    gl = gel[fc % 2]
                    if hf == 0:
                        dcopy(halo_t[:, fc:fc + 1], ab.ap[:, 1031:1032], [ab.r((1031, 1032))], [halo.r((fc, fc + 1))])
                    for side, tok, col in ((0, T0 - 1, 7), (1, T0 + 1024, 8 + 1024)):
                        if tok < 0 or tok >= SEQ:
                            S.op(POOL, lambda e, ab=ab, col=col: e.memset(ab.ap[:, col:col + 1], 0.0), [],
                                 [ab.r((col, col + 1))])
                        elif side == 0:
                            dcopy(ab.ap[:, col:col + 1], halo_t[:, fc:fc + 1], [halo.r((fc, fc + 1))], [ab.r((col, col + 1))])
                        else:
                            b = gbank()
                            for kc in range(8):
                                mm(banks[b][:, 0:1], slot_k8(sa_s)[:, kc, 128 * ci:128 * ci + 128],
                                   xnT_t[:, kc, tok:tok + 1], kc == 0, kc == 7,
                                   [SL(sa_s), xnT.r((kc, kc + 1), (tok, tok + 1))], [PS(b)])
                            dcopy(ab.ap[:, col:col + 1], banks[b][:, 0:1], [PS(b)], [ab.r((col, col + 1))])
                    dts(cb_.ap, ab.ap[:, 8:8 + 1024], cwc(1, fc), vcol(VC_CB + l * NFC + fc), ALU.mult, ALU.add,
                        [ab.r((8, 1032))] + VECS_R, [cb_.all()])
                    dstt(cb_.ap, ab.ap[:, 7:7 + 1024], cwc(0, fc), cb_.ap, ALU.mult, ALU.add,
                         [ab.r((7, 1031)), cb_.all()] + VECS_R, [cb_.all()])
                    dstt(cb_.ap, ab.ap[:, 9:9 + 1024], cwc(2, fc), cb_.ap, ALU.mult, ALU.add,
                         [ab.r((9, 1033)), cb_.all()] + VECS_R, [cb_.all()])
                    act(gl.ap, cb_.ap, AF.Gelu_apprx_tanh, [cb_.all()], [gl.all()])

                def part_g(ci):
                    fc = 4 * fq + ci
                    gl = gel[fc % 2]
                    for tb in range(2):
                        b = hbank()
                        proj_F(b, slot_k8(sg_s)[:, :, 128 * ci:128 * ci + 128], 0, xnT_t, xnT, T0 + 512 * tb, wres=SL(sg_s))
                        dtt(actT.ap[:, fc, 512 * tb:512 * tb + 512], banks[b][:, :],
                            gl.ap[:, 512 * tb:512 * tb + 512], ALU.mult,
                            [PS(b), gl.r((512 * tb, 512 * tb + 512))],
                            [actT.r((fc, fc + 1), (512 * tb, 512 * tb + 512))])

                part_a(0)
                part_a2(0)
                for ci in range(nch):
                    if ci + 1 < nch:
                        part_a(ci + 1)
                    part_g(ci)
                    if ci + 1 < nch:
                        part_a2(ci + 1)

            def dn_view(sd):
                return slots_t[sd][:, 0:NFC * 128].rearrange("p (k n) -> p k n", k=NFC)

            def dn_loads(slots, mch):
                wload(slots[0], dn_view(slots[0]), w_dn_l[:, :, 128 * mch:128 * mch + 128])

            def dn_compute(slots, hf, mch):
                sd = slots[0]
                T0 = 1024 * hf
                wd_v = dn_view(sd)
                for tb in range(2):
                    t0 = T0 + 512 * tb
                    b = xbank()
                    for kc in range(NFC):
                        mm(banks[b][:, :], wd_v[:, kc, :], actT.ap[:, kc, 512 * tb:512 * tb + 512],
                           kc == 0, kc == NFC - 1,
                           [SL(sd), actT.r((kc, kc + 1), (512 * tb, 512 * tb + 512))], [PS(b)])
                    dtt(hT_t[:, mch, t0:t0 + 512], banks[b][:, :], hT_t[:, mch, t0:t0 + 512], ALU.add,
                        [PS(b), hT.r((mch, mch + 1), (t0, t0 + 512))], [hT.r((mch, mch + 1), (t0, t0 + 512))])

            ftile = [0]

            def with_final(fn, hf, is_dn=False):
                if is_dn and hf == 1 and l == n_layers - 1 and ftile[0] < 8:
                    t = ftile[0]
                    ftile[0] += 1
                    return lambda s: (final_tile(t), fn(s))
                return fn

            for hf in range(2):
                for fq in range(6):
                    job(2, (lambda s, fq=fq: up_loads(s, fq)),
                        with_final((lambda s, hf=hf, fq=fq: up_compute(s, hf, fq)), hf), "up")
                for mch in range(8):
                    job(1, (lambda s, mch=mch: dn_loads(s, mch)),
                        with_final((lambda s, hf=hf, mch=mch: dn_compute(s, hf, mch)), hf, True), "dn")

            if l + 1 < n_layers:
                spread(norm_block_tasks(0, VC_G1 + 8 * (l + 1)) + norm_block_tasks(1, VC_G1 + 8 * (l + 1)),
                       len(jobs) - 14, len(jobs))
                job(0, None, lambda _: run_tasks(norm_block_tasks(2, VC_G1 + 8 * (l + 1))
                                                  + norm_block_tasks(3, VC_G1 + 8 * (l + 1))), "norm1b23")

            def after_layer(_):
                if dbg:
                    dma(SP, dbg_d["l%d" % l], hT_t[:].rearrange("p a b -> p (a b)"), [hT.all()], [], "dbg")

            job(0, None, after_layer)

        for l in range(n_layers):
            add_layer(l)

        assigned = []
        for (ns, lo, co, nm) in jobs:
            assigned.append([next_slot() for _ in range(ns)])
        loaded = -1
        for i, (ns, lo, co, nm) in enumerate(jobs):
            while loaded + 1 < len(jobs) and sum(jobs[k][0] for k in range(i, loaded + 2)) <= 4:
                loaded += 1
                if jobs[loaded][1] is not None:
                    new_fill()
                    jobs[loaded][1](assigned[loaded])
            if loaded < i:
                loaded = i
                if lo is not None:
                    new_fill()
                    lo(assigned[i])
            ex = extras.get(i, [])
            S.tag = "job%d:%s.x" % (i, nm)
            if ex and ex[0][0] is not None:
                ex[0][0]()
            S.tag = "job%d:%s" % (i, nm)
            co(assigned[i])
            S.tag = "job%d:%s.y" % (i, nm)
            if ex and ex[0][1] is not None:
                ex[0][1]()
            run_tasks(ex[1:])
            S.tag = ""

        for t in range(16):
            if t not in final_done:
                final_tile(t)

        dkeys = S.prepare()
        global _MS_TABLE
        _MS_TABLE = {(m[1], m[2]): S.ops[i]["tag"] for i, m in enumerate(S.ms) if m is not None}
        sems = {e: es.enter_context(nc.semaphore("s_" + e)) for e in ENGINES}
        dsems = {k: es.enter_context(nc.semaphore("d_" + k)) for k in dkeys}
        block = es.enter_context(nc.Block())

        @block.sync
        def _(eng):
            S.emit_engine(SP, eng, sems, dsems)
            for k in ("outA0", "outA1", "outB0", "outB1") + (("dbg",) if dbg else ()):
                eng.wait_ge(dsems[k], S.final_counts[1][k])

        @block.scalar
        def _(eng):
            S.emit_engine(ACT, eng, sems, dsems)

        @block.vector
        def _(eng):
            S.emit_engine(DVE, eng, sems, dsems)

        @block.gpsimd
        def _(eng):
            S.emit_engine(POOL, eng, sems, dsems)

        @block.tensor
        def _(eng):
            S.emit_engine(PE, eng, sems, dsems)
    print('[kernel] ops', len(S.ops), S.final_counts, flush=True)
    return nc


_NC_CACHE = {}
_MS_TABLE = {}


def _run(x, shared, n_layers=DEPTH, final_norm=True, stop=None, dbg=False):
    key = (n_layers, final_norm, stop, dbg)
    if key not in _NC_CACHE:
        _NC_CACHE[key] = build_nc(n_layers, final_norm, stop, dbg)
    nc = _NC_CACHE[key]
    in_maps = [dict(shared, x=np.ascontiguousarray(x[b])) for b in range(8)]
    res = run_bass_kernel_spmd(nc, in_maps, core_ids=list(range(8)))
    if dbg:
        return np.stack([r["y"] for r in res.results], axis=0), res.results[0]
    return np.stack([r["y"] for r in res.results], axis=0)


def kernel(x, w_in, w_pool, pool_scale, w_a, w_b, w_o, norm1, norm2,
           w_up, conv_w, conv_b, w_down, rel_bias, norm_f):
    f = lambda a: np.ascontiguousarray(np.asarray(a, dtype=np.float32))
    shared = dict(
        w_in=f(w_in), w_pool=f(w_pool), w_a=f(w_a), w_b=f(w_b), w_o=f(w_o), w_up=f(w_up), w_down=f(w_down),
        vecs=_pack_vecs(f(norm1), f(norm2), f(pool_scale), f(conv_w), f(conv_b)),
        normf_bc=np.ascontiguousarray(np.broadcast_to(f(norm_f)[None, :], (128, 1024))),
        ebraw=_bias_tiles(f(rel_bias)),
        bmat=_band_mats(),
    )
    return _run(f(x), shared).astype(np.float32)
```

```python
import numpy as np
from contextlib import ExitStack
import concourse.bass as bass
import concourse.mybir as mybir
from concourse.bass_utils import run_bass_kernel_spmd

F32 = mybir.dt.float32
BF16 = mybir.dt.bfloat16
AF = mybir.ActivationFunctionType
ALU = mybir.AluOpType

PE, ACT, DVE, POOL, SP = "pe", "act", "dve", "pool", "sp"
ENGINES = (PE, ACT, DVE, POOL, SP)

D_MODEL = 1024
SEQ = 2048
DEPTH = 2
NKC = 8
N_HEADS = 12
D_FF = 2816
NFC = 22
IN_WIDTH = 7680
OFF_Q, OFF_K, OFF_V, OFF_GP, OFF_GA = 1024, 2560, 4096, 5632, 6656
POOL_WINDOWS = (2, 4, 8, 16)
ATTN_GROUPS = ((128, 1), (512, 4), (2048, 16))
NEG_INF = -1e30
EPS = 1e-6
N_BUCKETS = 32
MAX_DISTANCE = 1024
VC_G1, VC_G2, VC_PS, VC_CW, VC_CB, VC_EPS, NV = 0, 16, 32, 48, 180, 224, 232
ARENA = 38720
NORM_OFF = 35648


class Sched:
    def __init__(self, same_engine_sync=(ACT, DVE, POOL)):
        self.ops = []
        self.same_engine_sync = set(same_engine_sync)

    tag = ""

    def op(self, eng, fn, reads=(), writes=(), dma=None, grp=None):
        self.ops.append(dict(eng=eng, fn=fn, reads=list(reads), writes=list(writes), dma=dma, tag=self.tag, grp=grp))

    @staticmethod
    def _overlap(a, b):
        if a is None or b is None:
            return True
        return not (a[1] <= b[0] or b[1] <= a[0])

    @staticmethod
    def _contains(a, b):
        if a is None:
            return True
        if b is None:
            return False
        return a[0] <= b[0] and b[1] <= a[1]

    def analyze(self):
        ops = self.ops
        live = {}
        deps = [set() for _ in ops]
        for i, o in enumerate(ops):
            eng = o["eng"]
            accs = [(r, "R") for r in o["reads"]] + [(w, "W") for w in o["writes"]]
            for (name, box), kind in accs:
                excl = name.startswith("ps:")
                lst = live.setdefault(name, [])
                for rbox, reng, ridx, rkind in lst:
                    if ridx == i:
                        continue
                    if excl:
                        deps[i].add(ridx)
                        continue
                    if kind == "R" and rkind == "R":
                        continue
                    if self._overlap(rbox, box):
                        deps[i].add(ridx)
            for (name, box), kind in accs:
                lst = live[name]
                excl = name.startswith("ps:")
                if kind == "W" or excl:
                    lst[:] = [r for r in lst if not self._contains(box, r[0])]
                    lst.append([box, eng, i, "W"])
                else:
                    rep = False
                    for r in lst:
                        if r[1] == eng and r[3] == "R" and r[0] == box:
                            r[2] = i
                            rep = True
                            break
                    if not rep:
                        lst.append([box, eng, i, "R"])
        self.deps = deps

    def _skip(self, oj, e):
        return oj["eng"] == e and oj["dma"] is None and e not in self.same_engine_sync

    def prepare(self):
        self.analyze()
        ops, deps = self.ops, self.deps
        needed = [False] * len(ops)
        for i, o in enumerate(ops):
            for j in deps[i]:
                if not self._skip(ops[j], o["eng"]):
                    needed[j] = True
        cnt = {e: 0 for e in ENGINES}
        dcnt = {}
        ms = [None] * len(ops)
        for i, o in enumerate(ops):
            if o["dma"] is not None:
                k = o["dma"]
                dcnt[k] = dcnt.get(k, 0) + 16
                ms[i] = ("d", k, dcnt[k])
            elif needed[i]:
                cnt[o["eng"]] += 1
                ms[i] = ("c", o["eng"], cnt[o["eng"]])
        gmax = {}
        for i, o in enumerate(ops):
            if o["dma"] is not None and o.get("grp") is not None:
                k = (o["dma"], o["grp"])
                gmax[k] = max(gmax.get(k, 0), ms[i][2])
        for i, o in enumerate(ops):
            if o["dma"] is not None and o.get("grp") is not None:
                ms[i] = ("d", o["dma"], gmax[(o["dma"], o["grp"])])
        self.ms = ms
        self.final_counts = (dict(cnt), dict(dcnt))
        self.streams = {e: [] for e in ENGINES}
        for i, o in enumerate(ops):
            self.streams[o["eng"]].append(i)
        return sorted(dcnt.keys())

    def emit_engine(self, e, eng, sems, dma_sems):
        ops, deps, ms = self.ops, self.deps, self.ms
        w = {}
        for i in self.streams[e]:
            o = ops[i]
            want = {}
            for j in deps[i]:
                if self._skip(ops[j], e):
                    continue
                m = ms[j]
                key = (m[0], m[1])
                if m[2] > want.get(key, 0):
                    want[key] = m[2]
            for key, val in want.items():
                if w.get(key, 0) >= val:
                    continue
                w[key] = val
                sem = dma_sems[key[1]] if key[0] == "d" else sems[key[1]]
                eng.wait_ge(sem, val)
            ins = o["fn"](eng)
            m = ms[i]
            if m is not None:
                if m[0] == "d":
                    ins.then_inc(dma_sems[m[1]], 16)
                else:
                    ins.then_inc(sems[m[1]], 1)


class Buf:
    def __init__(self, ap, shape, res, base=0, scale=1):
        self.ap, self.shape, self.res, self.base, self.scale = ap, tuple(shape), res, base, scale
        st, s = [], 1
        for d in reversed(self.shape):
            st.append(s)
            s *= d
        self.strides = tuple(reversed(st))
        self.n = s

    def r(self, *ranges):
        lo = hi = 0
        for k, d in enumerate(self.shape):
            a, b = ranges[k] if k < len(ranges) and ranges[k] is not None else (0, d)
            lo += a * self.strides[k]
            hi += (b - 1) * self.strides[k]
        return (self.res, (self.base + lo * self.scale, self.base + (hi + 1) * self.scale))

    def all(self):
        return (self.res, (self.base, self.base + self.n * self.scale))


def _t5_buckets(rel):
    n = -rel
    half = N_BUCKETS // 2
    ret = (n < 0).astype(np.int32) * half
    n = np.abs(n)
    max_exact = half // 2
    large = max_exact + (np.log(np.maximum(n, 1) / max_exact)
                         / np.log(MAX_DISTANCE / max_exact) * (half - max_exact)).astype(np.int32)
    large = np.minimum(large, half - 1)
    return (ret + np.where(n < max_exact, n, large)).astype(np.int32)


def _bias_tiles(rel_bias):
    out = np.full((128, 4, 640), NEG_INF, np.float32)
    i = np.arange(128)[:, None]
    for g, (window, dil) in enumerate(ATTN_GROUPS):
        half = window // (2 * dil)
        buckets = _t5_buckets(dil * np.arange(-half, half + 1))
        for h in range(4):
            col = rel_bias[:, g * 4 + h]
            def fill(dst, delta):
                valid = np.abs(delta) <= half
                j = np.clip(delta + half, 0, 2 * half)
                dst[...] = np.where(valid, col[buckets[j]], np.float32(NEG_INF))
            if g < 2:
                c = np.arange(128)[None, :]
                fill(out[:, h, g * 256:g * 256 + 128], i - c - 64)
                fill(out[:, h, g * 256 + 128:g * 256 + 256], i - c + 64)
            else:
                c = np.arange(32)[None, :]
                for m in range(4):
                    fill(out[:, h, 512 + 32 * m:512 + 32 * m + 32], i - 32 * m - c)
    return out.reshape(128, 2560)


def _band_mats():
    out = np.zeros((128, 20, 128), np.float32)
    for wi, w in enumerate(POOL_WINDOWS):
        def full(t_out0):
            M = np.zeros((3, 128, 128), np.float32)
            for c in range(128):
                t = t_out0 + c
                lo = max(t - w // 2, 0)
                hi = min(t - w // 2 + w, SEQ)
                cnt = hi - lo
                for ti in range(lo, hi):
                    d = ti // 128 - t_out0 // 128
                    M[d + 1, ti % 128, c] += 1.0 / cnt
                M[1, c, c] -= 1.0
            return M
        mi = full(128 * 5)
        out[:, wi * 5 + 0] = mi[0]
        out[:, wi * 5 + 1] = mi[1]
        out[:, wi * 5 + 2] = mi[2]
        out[:, wi * 5 + 3] = full(0)[1]
        out[:, wi * 5 + 4] = full(SEQ - 128)[1]
    return out.reshape(128, 2560)


def _chunked(v, nch):
    return np.ascontiguousarray(v.reshape(nch, 128).T)


def _pack_vecs(norm1, norm2, pool_scale, conv_w, conv_b):
    vecs = np.zeros((128, NV), np.float32)
    for l in range(DEPTH):
        vecs[:, VC_G1 + l * 8:VC_G1 + l * 8 + 8] = _chunked(norm1[l], 8)
        vecs[:, VC_G2 + l * 8:VC_G2 + l * 8 + 8] = _chunked(norm2[l], 8)
        vecs[:, VC_PS + l * 8:VC_PS + l * 8 + 8] = _chunked(pool_scale[l], 8)
        for j in range(3):
            o = VC_CW + (l * 3 + j) * NFC
            vecs[:, o:o + NFC] = _chunked(conv_w[l, j], NFC)
        vecs[:, VC_CB + l * NFC:VC_CB + l * NFC + NFC] = _chunked(conv_b[l], NFC)
    vecs[:, VC_EPS] = EPS
    return vecs


def build_nc(n_layers=DEPTH, final_norm=True, stop=None, dbg=False):
    nc = bass.Bass("TRN2", target_bir_lowering=False)
    dt_in = lambda n, s: nc.dram_tensor(n, s, F32, kind="ExternalInput").ap()
    x_d = dt_in("x", [SEQ, D_MODEL])
    w_in_d = dt_in("w_in", [DEPTH, D_MODEL, IN_WIDTH])
    w_pool_d = dt_in("w_pool", [DEPTH, 4, 256, 256])
    w_a_d = dt_in("w_a", [DEPTH, 1024, 1024])
    w_b_d = dt_in("w_b", [DEPTH, 512, 1024])
    w_o_d = dt_in("w_o", [DEPTH, 1024, 1024])
    w_up_d = dt_in("w_up", [DEPTH, 1024, 2 * D_FF])
    w_down_d = dt_in("w_down", [DEPTH, D_FF, 1024])
    vecs_d = dt_in("vecs", [128, NV])
    normf_d = dt_in("normf_bc", [128, 1024])
    ebraw_d = dt_in("ebraw", [128, 2560])
    bmat_d = dt_in("bmat", [128, 2560])
    y_d = nc.dram_tensor("y", [SEQ, D_MODEL], F32, kind="ExternalOutput").ap()
    dbg_d = {}
    if dbg:
        for l_ in range(n_layers):
            for nm in ("mix", "l"):
                dbg_d["%s%d" % (nm, l_)] = nc.dram_tensor("d_%s%d" % (nm, l_), [128, 8 * SEQ], F32, kind="ExternalOutput").ap()
            dbg_d["attn%d" % l_] = nc.dram_tensor("d_attn%d" % l_, [128, 4 * SEQ], F32, kind="ExternalOutput").ap()

    S = Sched()
    es = ExitStack()
    with es:
        sbt = lambda n, s, d: es.enter_context(nc.sbuf_tensor(n, s, d))
        hT_t = sbt("hT", [128, 8, SEQ], F32)
        xnT_t = sbt("xnT", [128, 8, SEQ], BF16)
        slots_t = [sbt("ws%d" % i, [128, 4096], BF16) for i in range(4)]
        arena_t = sbt("arena", [128, ARENA], BF16)
        ident_t = sbt("ident", [128, 128], F32)
        ones_t = sbt("onesb", [128, 128], BF16)
        invd_t = sbt("invdb", [128, 128], BF16)
        vecs_t = sbt("vecs_sb", [128, NV], F32)
        banks = [es.enter_context(nc.psum_tensor("pb%d" % i, [128, 512], F32)) for i in range(8)]

        hT = Buf(hT_t, (8, SEQ), "hT")
        xnT = Buf(xnT_t, (8, SEQ), "xnT")
        ident = Buf(ident_t, (128,), "ident")
        onesb = Buf(ones_t, (128,), "onesb")
        invdb = Buf(invd_t, (128,), "invdb")
        vecs = Buf(vecs_t, (NV,), "vecs")
        VECS_R = [vecs.all()]

        def vcol(c):
            return vecs_t[:, c:c + 1]

        class Arena:
            def __init__(self, start=0):
                self.off = start

            def take(self, shape, dt=BF16):
                n = int(np.prod(shape))
                ne = n * (2 if dt == F32 else 1)
                self.off = (self.off + 15) // 16 * 16
                assert self.off + ne <= ARENA, ("arena overflow", self.off, ne)
                v = arena_t[:, self.off:self.off + ne]
                if dt == F32:
                    v = v.bitcast(F32)
                if len(shape) == 2:
                    v = v.rearrange("p (a b) -> p a b", a=shape[0])
                elif len(shape) == 3:
                    v = v.rearrange("p (a b c) -> p a b c", a=shape[0], b=shape[1])
                b = Buf(v, shape, "arena", self.off, 2 if dt == F32 else 1)
                self.off += ne
                return b

        gb = [0]
        hb = [0]

        def gbank():
            gb[0] = (gb[0] + 1) % 4
            return gb[0]

        def hbank():
            hb[0] = (hb[0] + 1) % 3
            return 4 + hb[0]

        NORM_BANK = 7

        ab_ = [0]

        def abank():
            ab_[0] = (ab_[0] + 1) % 6
            return ab_[0]

        xb_ = [0]

        def xbank():
            xb_[0] = (xb_[0] + 1) % 7
            return xb_[0]

        def PS(b):
            return ("ps:%d" % b, None)

        def mm(out, lhsT, rhs, start, stop, reads, writes, skip=False):
            kw = dict(skip_group_check=True) if skip else {}
            S.op(PE, lambda e: e.matmul(out, lhsT=lhsT, rhs=rhs, start=start, stop=stop, **kw), reads, writes)

        def tr(out, in_, reads, writes):
            S.op(PE, lambda e: e.transpose(out=out, in_=in_, identity=ident_t[:]), reads + [ident.all()], writes)

        def act(out, in_, func, reads, writes, **kw):
            S.op(ACT, lambda e: e.activation(out=out, in_=in_, func=func, **kw), reads, writes)

        def dcopy(out, in_, reads, writes):
            S.op(DVE, lambda e: e.tensor_copy(out=out, in_=in_), reads, writes)

        tog = [0]

        def evac(out, in_, reads, writes):
            tog[0] ^= 1
            if tog[0]:
                act(out, in_, AF.Copy, reads, writes)
            else:
                dcopy(out, in_, reads, writes)

        def dtt(out, in0, in1, op, reads, writes):
            S.op(DVE, lambda e: e.tensor_tensor(out=out, in0=in0, in1=in1, op=op), reads, writes)

        def dstt(out, in0, scalar, in1, op0, op1, reads, writes):
            S.op(DVE, lambda e: e.scalar_tensor_tensor(out=out, in0=in0, scalar=scalar, in1=in1, op0=op0, op1=op1),
                 reads, writes)

        def drecip(out, in_, reads, writes):
            S.op(DVE, lambda e: e.reciprocal(out=out, in_=in_), reads, writes)

        def ptt(out, in0, in1, op, reads, writes):
            S.op(POOL, lambda e: e.tensor_tensor(out=out, in0=in0, in1=in1, op=op), reads, writes)

        def pts(out, in0, s1, s2, op0, op1, reads, writes):
            S.op(POOL, lambda e: e.tensor_scalar(out=out, in0=in0, scalar1=s1, scalar2=s2, op0=op0, op1=op1),
                 reads, writes)

        def dts(out, in0, s1, s2, op0, op1, reads, writes):
            S.op(DVE, lambda e: e.tensor_scalar(out=out, in0=in0, scalar1=s1, scalar2=s2, op0=op0, op1=op1),
                 reads, writes)

        def dma(q, out, in_, reads, writes, key, grp=None):
            S.op(q, lambda e: e.dma_start(out=out, in_=in_), reads, writes, dma=key, grp=grp)

        slot_i = [0]

        def next_slot():
            slot_i[0] = (slot_i[0] + 1) % 4
            return slot_i[0]

        def SL(i):
            return ("ws%d" % i, None)

        fill_id = [0]

        def new_fill():
            fill_id[0] += 1

        def wload(si, dst, src, part=None):
            box = None if part is None else (part, part + 1)
            dma(POOL, dst, src, [], [("ws%d" % si, box)], "ws%d" % si, grp=(si, fill_id[0]))

        def slot_k8(si, ncols=512):
            return slots_t[si][:, 0:8 * ncols].rearrange("p (k n) -> p k n", k=8)

        S.op(DVE, lambda e: e.memset(ident_t[:], 1.0), [], [ident.all()])
        S.op(POOL, lambda e: e.affine_select(out=ident_t[:], in_=ident_t[:], pattern=[[-1, 128]],
                                               compare_op=ALU.is_equal, fill=0.0, base=0, channel_multiplier=1),
             [ident.all()], [ident.all()])
        S.op(DVE, lambda e: e.memset(ones_t[:], 1.0), [], [onesb.all()])
        S.op(DVE, lambda e: e.memset(invd_t[:], 1.0 / D_MODEL), [], [invdb.all()])
        dma(SP, vecs_t[:], vecs_d, [], VECS_R, "vecs")

        jobs = []

        def job(nslots, loads, compute, name=""):
            jobs.append((nslots, loads, compute, name))

        def input_compute(_):
            ar = Arena()
            xin = [ar.take((1024,), F32) for _ in range(16)]
            for t in range(16):
                dma(SP if t % 2 == 0 else ACT, xin[t].ap, x_d[128 * t:128 * t + 128, :], [], [xin[t].all()], "xin%d" % t)
            for t in range(16):
                xb = xin[t]
                for hf in range(2):
                    b = xbank()
                    for j in range(4):
                        c = hf * 4 + j
                        tr(banks[b][:, 128 * j:128 * j + 128], xb.ap[:, 128 * c:128 * c + 128], [xb.all()], [PS(b)])
                    evac(hT_t[:, 4 * hf:4 * hf + 4, 128 * t:128 * t + 128],
                         banks[b][:].rearrange("p (a b) -> p a b", a=4),
                         [PS(b)], [hT.r((4 * hf, 4 * hf + 4), (128 * t, 128 * t + 128))])
                if t % 4 == 3 and n_layers > 0:
                    run_tasks(norm_block_tasks(t // 4, VC_G1))

        job(0, None, input_compute, "input")

        _nar = Arena(NORM_OFF)
        n_sq = [_nar.take((512,), BF16) for _ in range(2)]
        n_rs = [_nar.take((512,), F32) for _ in range(2)]

        def norm_block_tasks(blk, gcol, out_t=None, out_buf=None):
            out_t = xnT_t if out_t is None else out_t
            out_buf = xnT if out_buf is None else out_buf
            t0 = 512 * blk
            b = NORM_BANK
            r = n_rs[blk % 2]

            def t_sq(c0):
                def pre():
                    for c in (c0, c0 + 1):
                        s_ = n_sq[c % 2]
                        act(s_.ap, hT_t[:, c, t0:t0 + 512], AF.Square, [hT.r((c, c + 1), (t0, t0 + 512))], [s_.all()])

                def post():
                    for c in (c0, c0 + 1):
                        s_ = n_sq[c % 2]
                        mm(banks[b][:, :], invd_t[:], s_.ap, c == 0, c == 7, [invdb.all(), s_.all()], [PS(b)])
                return (pre, post)

            def t_rstd():
                act(r.ap, banks[b][:, :], AF.Ln, [PS(b)] + VECS_R, [r.all()], bias=vcol(VC_EPS), scale=1.0)
                act(r.ap, r.ap, AF.Exp, [r.all()], [r.all()], scale=-0.5)

            def t_xn(c0):
                def f():
                    for c in (c0, c0 + 1):
                        dstt(out_t[:, c, t0:t0 + 512], hT_t[:, c, t0:t0 + 512], vcol(gcol + c), r.ap, ALU.mult, ALU.mult,
                             [hT.r((c, c + 1), (t0, t0 + 512)), r.all()] + VECS_R,
                             [out_buf.r((c, c + 1), (t0, t0 + 512))])
                return (f, None)

            return [t_sq(0), t_sq(2), t_sq(4), t_sq(6), (None, t_rstd), t_xn(0), t_xn(2), t_xn(4), t_xn(6)]

        def run_tasks(tasks):
            for (pre, post) in tasks:
                if pre is not None:
                    pre()
                if post is not None:
                    post()

        extras = {}

        def spread(tasks, j0, j1):
            n = j1 - j0
            for k, t in enumerate(tasks):
                extras.setdefault(j0 + (k * n) // len(tasks), []).append(t)

        def proj_F(dst_bank, wslot_ap, col0, src_t, src_buf, t0, nk=8, wres=None):
            for kc in range(nk):
                mm(banks[dst_bank][:, :], wslot_ap[:, kc, col0:col0 + 128], src_t[:, kc, t0:t0 + 512],
                   kc == 0, kc == nk - 1,
                   [wres, src_buf.r((kc, kc + 1), (t0, t0 + 512))], [PS(dst_bank)])

        def tiles_of(g):
            res = []
            if g == 0:
                for J in range(17):
                    b0, b1 = max(128 * J - 64, 0), min(128 * J + 64, SEQ)
                    res.append((J, b0 - (128 * J - 64), b1 - b0, b0, 1))
            elif g == 1:
                for r in range(4):
                    for J in range(5):
                        b0, b1 = max(128 * J - 64, 0), min(128 * J + 64, 512)
                        res.append((r * 5 + J, b0 - (128 * J - 64), b1 - b0, r + 4 * b0, 4))
            else:
                for r in range(16):
                    res.append((r, 0, 128, r, 16))
            return res

        def xnf32(c):
            return Buf(xnT_t[:, c, 0:1024].bitcast(F32), (512,), "xnT", base=c * SEQ, scale=2)

        ost_lo, ost_hi = [xnf32(0), xnf32(6)], [xnf32(1), xnf32(7)]
        nf_lo, nf_hi, sqs_lo, sqs_hi = xnf32(2), xnf32(3), xnf32(4), xnf32(5)
        rs2_t = sbt("rs2", [128, 4], F32)
        halo_t = sbt("halo", [128, NFC], F32)
        halo = Buf(halo_t, (NFC,), "halo")
        rs2 = Buf(rs2_t, (4,), "rs2")
        final_done = []

        def final_tile(t):
            if not final_done:
                dma(SP, nf_lo.ap, normf_d[:, 0:512], [], [nf_lo.all()], "nf0")
                dma(SP, nf_hi.ap, normf_d[:, 512:1024], [], [nf_hi.all()], "nf1")
            final_done.append(t)
            ol, oh = ost_lo[t % 2], ost_hi[t % 2]
            bl, bh = xbank(), xbank()
            for c in range(8):
                b = bl if c < 4 else bh
                j = c % 4
                tr(banks[b][:, 128 * j:128 * j + 128], hT_t[:, c, 128 * t:128 * t + 128],
                   [hT.r((c, c + 1), (128 * t, 128 * t + 128))], [PS(b)])
            if final_norm:
                act(sqs_lo.ap, banks[bl][:, :], AF.Square, [PS(bl)], [sqs_lo.all()])
                act(sqs_hi.ap, banks[bh][:, :], AF.Square, [PS(bh)], [sqs_hi.all()])
                S.op(DVE, lambda e: e.reduce_sum(out=rs2_t[:, 0:1], in_=sqs_lo.ap, axis=mybir.AxisListType.X),
                     [sqs_lo.all()], [rs2.r((0, 1))])
                S.op(DVE, lambda e: e.reduce_sum(out=rs2_t[:, 1:2], in_=sqs_hi.ap, axis=mybir.AxisListType.X),
                     [sqs_hi.all()], [rs2.r((1, 2))])
                dtt(rs2_t[:, 2:3], rs2_t[:, 0:1], rs2_t[:, 1:2], ALU.add, [rs2.r((0, 2))], [rs2.r((2, 3))])
                act(rs2_t[:, 3:4], rs2_t[:, 2:3], AF.Ln, [rs2.r((2, 3))] + VECS_R, [rs2.r((3, 4))],
                    bias=vcol(VC_EPS), scale=1.0 / D_MODEL)
                act(rs2_t[:, 3:4], rs2_t[:, 3:4], AF.Exp, [rs2.r((3, 4))], [rs2.r((3, 4))], scale=-0.5)
                dstt(ol.ap, banks[bl][:, :], rs2_t[:, 3:4], nf_lo.ap, ALU.mult, ALU.mult,
                     [PS(bl), rs2.r((3, 4)), nf_lo.all()], [ol.all()])
                dstt(oh.ap, banks[bh][:, :], rs2_t[:, 3:4], nf_hi.ap, ALU.mult, ALU.mult,
                     [PS(bh), rs2.r((3, 4)), nf_hi.all()], [oh.all()])
            else:
                dcopy(ol.ap, banks[bl][:, :], [PS(bl)], [ol.all()])
                dcopy(oh.ap, banks[bh][:, :], [PS(bh)], [oh.all()])
            dma(SP, y_d[128 * t:128 * t + 128, 0:512], ol.ap, [ol.all()], [], "outA%d" % (t % 2))
            dma(SP, y_d[128 * t:128 * t + 128, 512:1024], oh.ap, [oh.all()], [], "outB%d" % (t % 2))

        def add_layer(l):
            w_in_l = w_in_d[l].rearrange("(kc p) n -> p kc n", p=128)
            w_a_l = w_a_d[l].rearrange("(kc p) n -> p kc n", p=128)
            w_b_l = w_b_d[l].rearrange("(kc p) n -> p kc n", p=128)
            w_o_l = w_o_d[l].rearrange("(kc p) n -> p kc n", p=128)
            w_up_l = w_up_d[l].rearrange("(kc p) n -> p kc n", p=128)
            w_dn_l = w_down_d[l].rearrange("(kc p) n -> p kc n", p=128)

            ar = Arena()
            attnT = ar.take((4, SEQ))
            a_mark = ar.off
            ar_n1 = Arena(a_mark)
            ar = Arena(a_mark)
            kT = ar.take((3, SEQ))
            V0 = ar.take((17, 128))
            V1 = ar.take((20, 128))
            V2 = ar.take((16, 128))
            qTb = [ar.take((3, 512)) for _ in range(3)]
            ptr = [ar.take((512,)) for _ in range(5)]
            pt = [ar.take((512,)) for _ in range(5)]
            EB = ar.take((4, 640))
            ebr = ar.take((640,), F32)
            rden = ar.take((512,), F32)
            Vg = (V0, V1, V2)

            def norm1_compute(_):
                for h in range(4):
                    dma(SP, ebr.ap, ebraw_d[:, 640 * h:640 * h + 640], [], [ebr.all()], "ebr")
                    act(EB.ap[:, h, :], ebr.ap, AF.Exp, [ebr.all()], [EB.r((h, h + 1))])

            job(0, None, norm1_compute, "norm1")

            def kv_loads(slots, h):
                for si, off in ((slots[0], OFF_K), (slots[1], OFF_V)):
                    for g in range(3):
                        c0 = off + (g * 4 + h) * 128
                        wload(si, slot_k8(si)[:, :, 128 * g:128 * g + 128], w_in_l[:, :, c0:c0 + 128], part=g)

            def kv_compute(slots, h):
                sk_, sv_ = slots
                wk, wv = slot_k8(sk_), slot_k8(sv_)
                for g in range(3):
                    for blk in range(4):
                        b = xbank()
                        proj_F(b, wk, 128 * g, xnT_t, xnT, 512 * blk, wres=SL(sk_))
                        if g == 0:
                            evac(kT.ap[:, g, 512 * blk:512 * blk + 512], banks[b][:, :], [PS(b)],
                                 [kT.r((g, g + 1), (512 * blk, 512 * blk + 512))])
                        else:
                            d_ = 4 if g == 1 else 16
                            nb_ = 512 // d_
                            evac(kT.ap[:, g, :].rearrange("p (r b) -> p b r", r=d_)[:, nb_ * blk:nb_ * blk + nb_, :],
                                 banks[b][:, :].rearrange("p (b r) -> p b r", r=d_), [PS(b)], [kT.r((g, g + 1))])
                for g in range(3):
                    tl = tiles_of(g)
                    i = 0
                    while i < len(tl):
                        n = 1
                        if tl[i][2] == 128:
                            while (n < 4 and i + n < len(tl) and tl[i + n][2] == 128
                                   and tl[i + n][0] == tl[i][0] + n):
                                n += 1
                        b = xbank()
                        for q_ in range(n):
                            (ti, row0, cnt, tok0, step) = tl[i + q_]
                            for kc in range(8):
                                mm(banks[b][row0:row0 + cnt, 128 * q_:128 * q_ + 128],
                                   xnT_t[:, kc, tok0:tok0 + step * (cnt - 1) + 1:step],
                                   wv[:, kc, 128 * g:128 * g + 128], kc == 0, kc == 7,
                                   [SL(sv_), xnT.r((kc, kc + 1))], [PS(b)])
                        (ti, row0, cnt, tok0, step) = tl[i]
                        if n == 1:
                            evac(Vg[g].ap[row0:row0 + cnt, ti, :], banks[b][row0:row0 + cnt, 0:128], [PS(b)],
                                 [Vg[g].r((ti, ti + 1))])
                        else:
                            evac(Vg[g].ap[:, ti:ti + n, :],
                                 banks[b][:, 0:128 * n].rearrange("p (a b) -> p a b", a=n), [PS(b)],
                                 [Vg[g].r((ti, ti + n))])
                        i += n

            def q_loads(slots, h):
                si = slots[0]
                for g in range(3):
                    c0 = OFF_Q + (g * 4 + h) * 128
                    wload(si, slot_k8(si)[:, :, 128 * g:128 * g + 128], w_in_l[:, :, c0:c0 + 128], part=g)

            def q_compute(slots, h):
                sq_ = slots[0]
                wq = slot_k8(sq_)
                base_tag = S.tag

                early = [4, 5, 6, 7, 0, 1]

                def qproj(m):
                    S.tag = base_tag + ".qproj%d" % m
                    qb = qTb[m % 3]
                    for g in range(3):
                        b = early[3 * m + g] if m < 2 else gbank()
                        proj_F(b, wq, 128 * g, xnT_t, xnT, 512 * m, wres=SL(sq_))
                        if g == 0:
                            evac(qb.ap[:, g, :], banks[b][:, :], [PS(b)], [qb.r((g, g + 1))])
                        else:
                            d_ = 4 if g == 1 else 16
                            evac(qb.ap[:, g, :].rearrange("p (r c) -> p c r", r=d_),
                                 banks[b][:, :].rearrange("p (c r) -> p c r", r=d_), [PS(b)], [qb.r((g, g + 1))])

                def block_groups(m):
                    qb = qTb[m % 3]
                    nb, db = (4, 5) if m % 2 == 0 else (6, 7)
                    groups = []
                    for up in range(2):
                        items = []
                        for s_ in range(4):
                            J = 4 * m + s_ + up
                            b0, b1 = max(128 * J - 64, 0), min(128 * J + 64, SEQ)
                            row0, cnt = b0 - (128 * J - 64), b1 - b0
                            items.append((128 * s_, row0, cnt, kT.ap[:, 0, b0:b1], qb.ap[:, 0, 128 * s_:128 * s_ + 128],
                                          V0.ap[row0:row0 + cnt, J, :],
                                          banks[nb][:, 128 * s_:128 * s_ + 128], banks[db][:, 128 * s_:128 * s_ + 128]))
                        groups.append((items, (128 * up, 128 * up + 128), 4, 128))
                    for up in range(2):
                        items = []
                        for r in range(4):
                            J = m + up
                            b0, b1 = max(128 * J - 64, 0), min(128 * J + 64, 512)
                            row0, cnt = b0 - (128 * J - 64), b1 - b0
                            items.append((128 * r, row0, cnt,
                                          kT.ap[:, 1, 512 * r + b0:512 * r + b1],
                                          qb.ap[:, 1, 128 * r:128 * r + 128],
                                          V1.ap[row0:row0 + cnt, r * 5 + J, :],
                                          banks[nb][:, r:512:4], banks[db][:, r:512:4]))
                        groups.append((items, (256 + 128 * up, 256 + 128 * up + 128), 4, 128))
                    items = []
                    for r in range(16):
                        items.append((32 * r, 0, 128, kT.ap[:, 2, 128 * r:128 * r + 128], qb.ap[:, 2, 32 * r:32 * r + 32],
                                      V2.ap[:, r, :], banks[nb][:, r:512:16], banks[db][:, r:512:16]))
                    groups.append((items, (512 + 32 * m, 512 + 32 * m + 32), 16, 32))
                    return groups

                seq = []
                for m in range(4):
                    for gi, G in enumerate(block_groups(m)):
                        seq.append((m, gi, G))
                pidx = [0]

                def scores(n):
                    m, gi, (items, ebcols, nrep, ncol) = seq[n]
                    S.tag = base_tag + ".m%d.sc%d" % (m, gi)
                    qb = qTb[m % 3]
                    b = gbank()
                    for (c0, row0, cnt, lT, rh, vap, no, do) in items:
                        mm(banks[b][row0:row0 + cnt, c0:c0 + ncol], lT, rh, True, True,
                           [kT.all(), qb.all()], [PS(b)])
                    k = n % 5
                    act(ptr[k].ap, banks[b][:, :], AF.Exp, [PS(b)], [ptr[k].all()], scale=float(128 ** -0.5))
                    ebv = EB.ap[:, h, ebcols[0]:ebcols[1]].unsqueeze(1).broadcast_to([128, nrep, ncol])
                    dtt(pt[k].ap.rearrange("p (a b) -> p a b", a=nrep),
                        ptr[k].ap.rearrange("p (a b) -> p a b", a=nrep), ebv, ALU.mult,
                        [ptr[k].all(), EB.r((h, h + 1))], [pt[k].all()])

                def pv(n):
                    m, gi, (items, ebcols, nrep, ncol) = seq[n]
                    S.tag = base_tag + ".m%d.pv%d" % (m, gi)
                    nb, db = (4, 5) if m % 2 == 0 else (6, 7)
                    k = n % 5
                    first = (gi == 0)
                    for (c0, row0, cnt, lT, rh, vap, no, do) in items:
                        mm(no, vap, pt[k].ap[row0:row0 + cnt, c0:c0 + ncol], first, False,
                           [Vg[0].all(), Vg[1].all(), Vg[2].all(), pt[k].all()], [PS(nb)], skip=True)
                        mm(do, ones_t[row0:row0 + cnt, :], pt[k].ap[row0:row0 + cnt, c0:c0 + ncol], first, False,
                           [onesb.all(), pt[k].all()], [PS(db)], skip=True)
                        first = False
                    if gi == 4:
                        S.tag = base_tag + ".m%d.norm" % m
                        act(rden.ap, banks[db][:, :], AF.Ln, [PS(db)], [rden.all()])
                        act(rden.ap, rden.ap, AF.Exp, [rden.all()], [rden.all()], scale=-1.0)
                        dtt(attnT.ap[:, h, 512 * m:512 * m + 512], banks[nb][:, :], rden.ap, ALU.mult,
                            [PS(nb), rden.all()], [attnT.r((h, h + 1), (512 * m, 512 * m + 512))])

                LOOK = 4
                qproj(0)
                qproj(1)
                for n in range(LOOK):
                    scores(n)
                qproj(2)
                for n in range(len(seq)):
                    m, gi, _ = seq[n]
                    if gi == 1 and m == 1:
                        qproj(3)
                    pv(n)
                    if n + LOOK < len(seq):
                        scores(n + LOOK)

            for h in range(4):
                job(2, (lambda s, h=h: kv_loads(s, h)), (lambda s, h=h: kv_compute(s, h)), "kv%d" % h)
                job(1, (lambda s, h=h: q_loads(s, h)), (lambda s, h=h: q_compute(s, h)), "q%d" % h)

            ar = Arena(a_mark)
            bmat = ar.take((20, 128))
            u_sb = ar.take((9, 256))
            pooledT = ar.take((2, 1024))
            pool_outT = ar.take((8, 1024))
            merged = ar.take((8, 1024))
            sgp = [ar.take((512,)) for _ in range(2)]
            sga = [ar.take((512,)) for _ in range(2)]
            t1 = ar.take((512,), F32)
            t2 = ar.take((512,), F32)

            def pool_loads(slots, g):
                si = slots[0]
                wload(si, slots_t[si][:, 0:2048].rearrange("p (k n) -> p k n", k=8), w_in_l[:, :, 256 * g:256 * g + 256],
                      part=0)
                wload(si, slots_t[si][:, 2048:2560].rearrange("p (k n) -> p k n", k=2),
                      w_pool_d[l, g].rearrange("(kc p) n -> p kc n", p=128), part=1)

            def pool_compute(slots, hf, g):
                si = slots[0]
                wu = slots_t[si][:, 0:2048].rearrange("p (k n) -> p k n", k=8)
                wp = slots_t[si][:, 2048:2560].rearrange("p (k n) -> p k n", k=2)
                tl0 = max(8 * hf - 1, 0)
                tl1 = min(8 * hf + 9, 16)
                if hf == 0 and g == 0:
                    if dbg:
                        dma(POOL, dbg_d["attn%d" % l], attnT.ap.rearrange("p a b -> p (a b)"), [attnT.all()], [], "dbg")
                    dma(POOL, bmat.ap.rearrange("p a b -> p (a b)"), bmat_d, [], [bmat.all()], "bmat")
                for t in range(tl0, tl1):
                    b = xbank()
                    for kc in range(8):
                        mm(banks[b][:, 0:256], xnT_t[:, kc, 128 * t:128 * t + 128], wu[:, kc, :], kc == 0, kc == 7,
                           [SL(si), xnT.r((kc, kc + 1), (128 * t, 128 * t + 128))], [PS(b)])
                    evac(u_sb.ap[:, t - tl0, :], banks[b][:, 0:256], [PS(b)], [u_sb.r((t - tl0, t - tl0 + 1))])
                for cc in range(2):
                    for tb in range(2):
                        b = xbank()
                        for j in range(4):
                            t = 8 * hf + 4 * tb + j
                            terms = []
                            if t > 0:
                                terms.append((t - 1, 5 * g + 0))
                            terms.append((t, 5 * g + (3 if t == 0 else 4 if t == 15 else 1)))
                            if t < 15:
                                terms.append((t + 1, 5 * g + 2))
                            for k, (tin, bi) in enumerate(terms):
                                mm(banks[b][:, 128 * j:128 * j + 128],
                                   u_sb.ap[:, tin - tl0, 128 * cc:128 * cc + 128], bmat.ap[:, bi, :],
                                   k == 0, k == len(terms) - 1,
                                   [u_sb.r((tin - tl0, tin - tl0 + 1)), bmat.all()], [PS(b)])
                        evac(pooledT.ap[:, cc, 512 * tb:512 * tb + 512], banks[b][:, :], [PS(b)],
                             [pooledT.r((cc, cc + 1), (512 * tb, 512 * tb + 512))])
                for dc in range(2):
                    for tb in range(2):
                        b = xbank()
                        for kc in range(2):
                            mm(banks[b][:, :], wp[:, kc, 128 * dc:128 * dc + 128],
                               pooledT.ap[:, kc, 512 * tb:512 * tb + 512], kc == 0, kc == 1,
                               [SL(si), pooledT.r((kc, kc + 1), (512 * tb, 512 * tb + 512))], [PS(b)])
                        ch = 2 * g + dc
                        act(pool_outT.ap[:, ch, 512 * tb:512 * tb + 512], banks[b][:, :], AF.Identity,
                            [PS(b)] + VECS_R, [pool_outT.r((ch, ch + 1), (512 * tb, 512 * tb + 512))],
                            scale=vcol(VC_PS + 8 * l + ch))

            def mview(si, part):
                if part < 3:
                    return slots_t[si][:, 1024 * part:1024 * part + 1024].rearrange("p (k n) -> p k n", k=8)
                return slots_t[si][:, 3072:3584].rearrange("p (k n) -> p k n", k=4)

            def merge_loads(slots, mch):
                si = slots[0]
                wload(si, mview(si, 0), w_in_l[:, :, OFF_GP + 128 * mch:OFF_GP + 128 * mch + 128], part=0)
                wload(si, mview(si, 1), w_in_l[:, :, OFF_GA + 128 * mch:OFF_GA + 128 * mch + 128], part=1)
                wload(si, mview(si, 2), w_a_l[:, :, 128 * mch:128 * mch + 128], part=2)
                wload(si, mview(si, 3), w_b_l[:, :, 128 * mch:128 * mch + 128], part=3)

            def merge_compute(slots, hf, mch):
                si = slots[0]
                T0 = 1024 * hf
                wgp, wga, wa, wb = mview(si, 0), mview(si, 1), mview(si, 2), mview(si, 3)
                for tb in range(2):
                    t0 = T0 + 512 * tb
                    gp, ga = sgp[tb], sga[tb]
                    b = gbank()
                    proj_F(b, wgp, 0, xnT_t, xnT, t0, wres=SL(si))
                    act(gp.ap, banks[b][:, :], AF.Sigmoid, [PS(b)], [gp.all()])
                    b = gbank()
                    proj_F(b, wga, 0, xnT_t, xnT, t0, wres=SL(si))
                    act(ga.ap, banks[b][:, :], AF.Sigmoid, [PS(b)], [ga.all()])
                    b = hbank()
                    for kc in range(8):
                        mm(banks[b][:, :], wa[:, kc, :], pool_outT.ap[:, kc, 512 * tb:512 * tb + 512], kc == 0, kc == 7,
                           [SL(si), pool_outT.r((kc, kc + 1), (512 * tb, 512 * tb + 512))], [PS(b)])
                    dtt(t1.ap, banks[b][:, :], gp.ap, ALU.mult, [PS(b), gp.all()], [t1.all()])
                    b = hbank()
                    for kc in range(4):
                        mm(banks[b][:, :], wb[:, kc, :], attnT.ap[:, kc, t0:t0 + 512], kc == 0, kc == 3,
                           [SL(si), attnT.r((kc, kc + 1), (t0, t0 + 512))], [PS(b)])
                    dtt(t2.ap, banks[b][:, :], ga.ap, ALU.mult, [PS(b), ga.all()], [t2.all()])
                    ptt(merged.ap[:, mch, 512 * tb:512 * tb + 512], t1.ap, t2.ap, ALU.add,
                        [t1.all(), t2.all()], [merged.r((mch, mch + 1), (512 * tb, 512 * tb + 512))])

            def wo_loads(slots, mq):
                si = slots[0]
                wload(si, slot_k8(si), w_o_l[:, :, 512 * mq:512 * mq + 512])

            def wo_compute(slots, hf, mq):
                so = slots[0]
                T0 = 1024 * hf
                for mi in range(4):
                    mch = 4 * mq + mi
                    for tb in range(2):
                        t0 = T0 + 512 * tb
                        b = xbank()
                        for kc in range(8):
                            mm(banks[b][:, :], slot_k8(so)[:, kc, 128 * mi:128 * mi + 128],
                               merged.ap[:, kc, 512 * tb:512 * tb + 512], kc == 0, kc == 7,
                               [SL(so), merged.r((kc, kc + 1), (512 * tb, 512 * tb + 512))], [PS(b)])
                        dtt(hT_t[:, mch, t0:t0 + 512], banks[b][:, :], hT_t[:, mch, t0:t0 + 512], ALU.add,
                            [PS(b), hT.r((mch, mch + 1), (t0, t0 + 512))], [hT.r((mch, mch + 1), (t0, t0 + 512))])

            def wo1_loads(slots):
                for mq in range(2):
                    wload(slots[mq], slot_k8(slots[mq]), w_o_l[:, :, 512 * mq:512 * mq + 512])

            def wo1_compute(slots):
                tasks = norm_block_tasks(2, VC_G2 + 8 * l)
                for tb in range(2):
                    t0 = 1024 + 512 * tb
                    for mch in range(8):
                        so, mi = slots[mch // 4], mch % 4
                        tk = tasks[mch] if tb == 1 else (None, None)
                        if tk[0] is not None:
                            tk[0]()
                        b = xbank()
                        for kc in range(8):
                            mm(banks[b][:, :], slot_k8(so)[:, kc, 128 * mi:128 * mi + 128],
                               merged.ap[:, kc, 512 * tb:512 * tb + 512], kc == 0, kc == 7,
                               [SL(so), merged.r((kc, kc + 1), (512 * tb, 512 * tb + 512))], [PS(b)])
                        dtt(hT_t[:, mch, t0:t0 + 512], banks[b][:, :], hT_t[:, mch, t0:t0 + 512], ALU.add,
                            [PS(b), hT.r((mch, mch + 1), (t0, t0 + 512))], [hT.r((mch, mch + 1), (t0, t0 + 512))])
                        if tk[1] is not None:
                            tk[1]()
                run_tasks(tasks[8:])

            for hf in range(2):
                if hf == 1:
                    jp = len(jobs)
                    spread(norm_block_tasks(0, VC_G2 + 8 * l), jp, jp + 9)
                    spread(norm_block_tasks(1, VC_G2 + 8 * l), jp + 9, jp + 13)
                for g in range(4):
                    job(1, (lambda s, g=g: pool_loads(s, g)), (lambda s, hf=hf, g=g: pool_compute(s, hf, g)), "pool")
                for mch in range(8):
                    job(1, (lambda s, mch=mch: merge_loads(s, mch)), (lambda s, hf=hf, mch=mch: merge_compute(s, hf, mch)), "merge")
                if hf == 0:
                    for mq in range(2):
                        job(1, (lambda s, mq=mq: wo_loads(s, mq)), (lambda s, hf=hf, mq=mq: wo_compute(s, hf, mq)), "wo")
                else:
                    job(2, wo1_loads, wo1_compute, "wo1")

            def after_mix(_):
                if dbg:
                    dma(SP, dbg_d["mix%d" % l], hT_t[:].rearrange("p a b -> p (a b)"), [hT.all()], [], "dbg")

            job(0, None, after_mix)
            if stop == 'mix':
                return

            ar = Arena()
            actT = ar.take((NFC, 1024))
            a_sb = [ar.take((1040,), F32) for _ in range(2)]
            cv = [ar.take((1024,), F32) for _ in range(2)]
            gel = [ar.take((1024,)) for _ in range(2)]
            ar_n2 = ar
            cwc = lambda j, c: vcol(VC_CW + (l * 3 + j) * NFC + c)

            j_up0 = len(jobs)
            spread(norm_block_tasks(3, VC_G2 + 8 * l), j_up0, j_up0 + 14)

            def up_loads(slots, fq):
                nch = 4 if fq < 5 else 2
                sa_s, sg_s = slots
                wload(sa_s, slot_k8(sa_s)[:, :, 0:128 * nch], w_up_l[:, :, 512 * fq:512 * fq + 128 * nch])
                wload(sg_s, slot_k8(sg_s)[:, :, 0:128 * nch], w_up_l[:, :, D_FF + 512 * fq:D_FF + 512 * fq + 128 * nch])

            def up_compute(slots, hf, fq):
                nch = 4 if fq < 5 else 2
                sa_s, sg_s = slots
                T0 = 1024 * hf

                def part_a(ci):
                    fc = 4 * fq + ci
                    ab = a_sb[fc % 2]
                    cb_ = cv[fc % 2]
                    gl = gel[fc % 2]
                    for tb in range(2):
                        b = gbank()
                        proj_F(b, slot_k8(sa_s)[:, :, 128 * ci:128 * ci + 128], 0, xnT_t, xnT, T0 + 512 * tb, wres=SL(sa_s))
                        act(ab.ap[:, 8 + 512 * tb:8 + 512 * tb + 512], banks[b][:, :], AF.Copy, [PS(b)],
                            [ab.r((8 + 512 * tb, 8 + 512 * tb + 512))])
                def part_a2(ci):
                    fc = 4 * fq + ci
                    ab = a_sb[fc % 2]
                    cb_ = cv[fc % 2]
                    gl = gel[fc % 2]
                    if hf == 0:
                        dcopy(halo_t[:, fc:fc + 1], ab.ap[:, 1031:1032], [ab.r((1031, 1032))], [halo.r((fc, fc + 1))])
                    for side, tok, col in ((0, T0 - 1, 7), (1, T0 + 1024, 8 + 1024)):
                        if tok < 0 or tok >= SEQ:
                            S.op(POOL, lambda e, ab=ab, col=col: e.memset(ab.ap[:, col:col + 1], 0.0), [],
                                 [ab.r((col, col + 1))])
                        elif side == 0:
                            dcopy(ab.ap[:, col:col + 1], halo_t[:, fc:fc + 1], [halo.r((fc, fc + 1))], [ab.r((col, col + 1))])
                        else:
                            b = gbank()
                            for kc in range(8):
                                mm(banks[b][:, 0:1], slot_k8(sa_s)[:, kc, 128 * ci:128 * ci + 128],
                                   xnT_t[:, kc, tok:tok + 1], kc == 0, kc == 7,
                                   [SL(sa_s), xnT.r((kc, kc + 1), (tok, tok + 1))], [PS(b)])
                            dcopy(ab.ap[:, col:col + 1], banks[b][:, 0:1], [PS(b)], [ab.r((col, col + 1))])
                    dts(cb_.ap, ab.ap[:, 8:8 + 1024], cwc(1, fc), vcol(VC_CB + l * NFC + fc), ALU.mult, ALU.add,
                        [ab.r((8, 1032))] + VECS_R, [cb_.all()])
                    dstt(cb_.ap, ab.ap[:, 7:7 + 1024], cwc(0, fc), cb_.ap, ALU.mult, ALU.add,
                         [ab.r((7, 1031)), cb_.all()] + VECS_R, [cb_.all()])
                    dstt(cb_.ap, ab.ap[:, 9:9 + 1024], cwc(2, fc), cb_.ap, ALU.mult, ALU.add,
                         [ab.r((9, 1033)), cb_.all()] + VECS_R, [cb_.all()])
                    act(gl.ap, cb_.ap, AF.Gelu_apprx_tanh, [cb_.all()], [gl.all()])

                def part_g(ci):
                    fc = 4 * fq + ci
                    gl = gel[fc % 2]
                    for tb in range(2):
                        b = hbank()
                        proj_F(b, slot_k8(sg_s)[:, :, 128 * ci:128 * ci + 128], 0, xnT_t, xnT, T0 + 512 * tb, wres=SL(sg_s))
                        dtt(actT.ap[:, fc, 512 * tb:512 * tb + 512], banks[b][:, :],
                            gl.ap[:, 512 * tb:512 * tb + 512], ALU.mult,
                            [PS(b), gl.r((512 * tb, 512 * tb + 512))],
                            [actT.r((fc, fc + 1), (512 * tb, 512 * tb + 512))])

                part_a(0)
                part_a2(0)
                for ci in range(nch):
                    if ci + 1 < nch:
                        part_a(ci + 1)
                    part_g(ci)
                    if ci + 1 < nch:
                        part_a2(ci + 1)

            def dn_view(sd):
                return slots_t[sd][:, 0:NFC * 128].rearrange("p (k n) -> p k n", k=NFC)

            def dn_loads(slots, mch):
                wload(slots[0], dn_view(slots[0]), w_dn_l[:, :, 128 * mch:128 * mch + 128])

            def dn_compute(slots, hf, mch):
                sd = slots[0]
                T0 = 1024 * hf
                wd_v = dn_view(sd)
                for tb in range(2):
                    t0 = T0 + 512 * tb
                    b = xbank()
                    for kc in range(NFC):
                        mm(banks[b][:, :], wd_v[:, kc, :], actT.ap[:, kc, 512 * tb:512 * tb + 512],
                           kc == 0, kc == NFC - 1,
                           [SL(sd), actT.r((kc, kc + 1), (512 * tb, 512 * tb + 512))], [PS(b)])
                    dtt(hT_t[:, mch, t0:t0 + 512], banks[b][:, :], hT_t[:, mch, t0:t0 + 512], ALU.add,
                        [PS(b), hT.r((mch, mch + 1), (t0, t0 + 512))], [hT.r((mch, mch + 1), (t0, t0 + 512))])

            ftile = [0]

            def with_final(fn, hf, is_dn=False):
                if is_dn and hf == 1 and l == n_layers - 1 and ftile[0] < 8:
                    t = ftile[0]
                    ftile[0] += 1
                    return lambda s: (final_tile(t), fn(s))
                return fn

            for hf in range(2):
                for fq in range(6):
                    job(2, (lambda s, fq=fq: up_loads(s, fq)),
                        with_final((lambda s, hf=hf, fq=fq: up_compute(s, hf, fq)), hf), "up")
                for mch in range(8):
                    job(1, (lambda s, mch=mch: dn_loads(s, mch)),
                        with_final((lambda s, hf=hf, mch=mch: dn_compute(s, hf, mch)), hf, True), "dn")

            if l + 1 < n_layers:
                spread(norm_block_tasks(0, VC_G1 + 8 * (l + 1)) + norm_block_tasks(1, VC_G1 + 8 * (l + 1)),
                       len(jobs) - 14, len(jobs))
                job(0, None, lambda _: run_tasks(norm_block_tasks(2, VC_G1 + 8 * (l + 1))
                                                  + norm_block_tasks(3, VC_G1 + 8 * (l + 1))), "norm1b23")

            def after_layer(_):
                if dbg:
                    dma(SP, dbg_d["l%d" % l], hT_t[:].rearrange("p a b -> p (a b)"), [hT.all()], [], "dbg")

            job(0, None, after_layer)

        for l in range(n_layers):
            add_layer(l)

        assigned = []
        for (ns, lo, co, nm) in jobs:
            assigned.append([next_slot() for _ in range(ns)])
        loaded = -1
        for i, (ns, lo, co, nm) in enumerate(jobs):
            while loaded + 1 < len(jobs) and sum(jobs[k][0] for k in range(i, loaded + 2)) <= 4:
                loaded += 1
                if jobs[loaded][1] is not None:
                    new_fill()
                    jobs[loaded][1](assigned[loaded])
            if loaded < i:
                loaded = i
                if lo is not None:
                    new_fill()
                    lo(assigned[i])
            ex = extras.get(i, [])
            S.tag = "job%d:%s.x" % (i, nm)
            if ex and ex[0][0] is not None:
                ex[0][0]()
            S.tag = "job%d:%s" % (i, nm)
            co(assigned[i])
            S.tag = "job%d:%s.y" % (i, nm)
            if ex and ex[0][1] is not None:
                ex[0][1]()
            run_tasks(ex[1:])
            S.tag = ""

        for t in range(16):
            if t not in final_done:
                final_tile(t)

        dkeys = S.prepare()
        global _MS_TABLE
        _MS_TABLE = {(m[1], m[2]): S.ops[i]["tag"] for i, m in enumerate(S.ms) if m is not None}
        sems = {e: es.enter_context(nc.semaphore("s_" + e)) for e in ENGINES}
        dsems = {k: es.enter_context(nc.semaphore("d_" + k)) for k in dkeys}
        block = es.enter_context(nc.Block())

        @block.sync
        def _(eng):
            S.emit_engine(SP, eng, sems, dsems)
            for k in ("outA0", "outA1", "outB0", "outB1") + (("dbg",) if dbg else ()):
                eng.wait_ge(dsems[k], S.final_counts[1][k])

        @block.scalar
        def _(eng):
            S.emit_engine(ACT, eng, sems, dsems)

        @block.vector
        def _(eng):
            S.emit_engine(DVE, eng, sems, dsems)

        @block.gpsimd
        def _(eng):
            S.emit_engine(POOL, eng, sems, dsems)

        @block.tensor
        def _(eng):
            S.emit_engine(PE, eng, sems, dsems)
    print('[kernel] ops', len(S.ops), S.final_counts, flush=True)
    return nc


_NC_CACHE = {}
_MS_TABLE = {}


def _run(x, shared, n_layers=DEPTH, final_norm=True, stop=None, dbg=False):
    key = (n_layers, final_norm, stop, dbg)
    if key not in _NC_CACHE:
        _NC_CACHE[key] = build_nc(n_layers, final_norm, stop, dbg)
    nc = _NC_CACHE[key]
    in_maps = [dict(shared, x=np.ascontiguousarray(x[b])) for b in range(8)]
    res = run_bass_kernel_spmd(nc, in_maps, core_ids=list(range(8)))
    if dbg:
        return np.stack([r["y"] for r in res.results], axis=0), res.results[0]
    return np.stack([r["y"] for r in res.results], axis=0)


def kernel(x, w_in, w_pool, pool_scale, w_a, w_b, w_o, norm1, norm2,
           w_up, conv_w, conv_b, w_down, rel_bias, norm_f):
    f = lambda a: np.ascontiguousarray(np.asarray(a, dtype=np.float32))
    shared = dict(
        w_in=f(w_in), w_pool=f(w_pool), w_a=f(w_a), w_b=f(w_b), w_o=f(w_o), w_up=f(w_up), w_down=f(w_down),
        vecs=_pack_vecs(f(norm1), f(norm2), f(pool_scale), f(conv_w), f(conv_b)),
        normf_bc=np.ascontiguousarray(np.broadcast_to(f(norm_f)[None, :], (128, 1024))),
        ebraw=_bias_tiles(f(rel_bias)),
        bmat=_band_mats(),
    )
    return _run(f(x), shared).astype(np.float32)
```

```python
import numpy as np
from contextlib import ExitStack
import concourse.bass as bass
import concourse.mybir as mybir
from concourse.bass_utils import run_bass_kernel_spmd

F32 = mybir.dt.float32
BF16 = mybir.dt.bfloat16
AF = mybir.ActivationFunctionType
ALU = mybir.AluOpType

PE, ACT, DVE, POOL, SP = "pe", "act", "dve", "pool", "sp"
ENGINES = (PE, ACT, DVE, POOL, SP)

D_MODEL = 1024
SEQ = 2048
DEPTH = 2
NKC = 8
N_HEADS = 12
D_FF = 2816
NFC = 22
IN_WIDTH = 7680
OFF_Q, OFF_K, OFF_V, OFF_GP, OFF_GA = 1024, 2560, 4096, 5632, 6656
POOL_WINDOWS = (2, 4, 8, 16)
ATTN_GROUPS = ((128, 1), (512, 4), (2048, 16))
NEG_INF = -1e30
EPS = 1e-6
N_BUCKETS = 32
MAX_DISTANCE = 1024
VC_G1, VC_G2, VC_PS, VC_CW, VC_CB, VC_EPS, NV = 0, 16, 32, 48, 180, 224, 232
ARENA = 38720
NORM_OFF = 35648


class Sched:
    def __init__(self, same_engine_sync=(ACT, DVE, POOL)):
        self.ops = []
        self.same_engine_sync = set(same_engine_sync)

    tag = ""

    def op(self, eng, fn, reads=(), writes=(), dma=None, grp=None):
        self.ops.append(dict(eng=eng, fn=fn, reads=list(reads), writes=list(writes), dma=dma, tag=self.tag, grp=grp))

    @staticmethod
    def _overlap(a, b):
        if a is None or b is None:
            return True
        return not (a[1] <= b[0] or b[1] <= a[0])

    @staticmethod
    def _contains(a, b):
        if a is None:
            return True
        if b is None:
            return False
        return a[0] <= b[0] and b[1] <= a[1]

    def analyze(self):
        ops = self.ops
        live = {}
        deps = [set() for _ in ops]
        for i, o in enumerate(ops):
            eng = o["eng"]
            accs = [(r, "R") for r in o["reads"]] + [(w, "W") for w in o["writes"]]
            for (name, box), kind in accs:
                excl = name.startswith("ps:")
                lst = live.setdefault(name, [])
                for rbox, reng, ridx, rkind in lst:
                    if ridx == i:
                        continue
                    if excl:
                        deps[i].add(ridx)
                        continue
                    if kind == "R" and rkind == "R":
                        continue
                    if self._overlap(rbox, box):
                        deps[i].add(ridx)
            for (name, box), kind in accs:
                lst = live[name]
                excl = name.startswith("ps:")
                if kind == "W" or excl:
                    lst[:] = [r for r in lst if not self._contains(box, r[0])]
                    lst.append([box, eng, i, "W"])
                else:
                    rep = False
                    for r in lst:
                        if r[1] == eng and r[3] == "R" and r[0] == box:
                            r[2] = i
                            rep = True
                            break
                    if not rep:
                        lst.append([box, eng, i, "R"])
        self.deps = deps

    def _skip(self, oj, e):
        return oj["eng"] == e and oj["dma"] is None and e not in self.same_engine_sync

    def prepare(self):
        self.analyze()
        ops, deps = self.ops, self.deps
        needed = [False] * len(ops)
        for i, o in enumerate(ops):
            for j in deps[i]:
                if not self._skip(ops[j], o["eng"]):
                    needed[j] = True
        cnt = {e: 0 for e in ENGINES}
        dcnt = {}
        ms = [None] * len(ops)
        for i, o in enumerate(ops):
            if o["dma"] is not None:
                k = o["dma"]
                dcnt[k] = dcnt.get(k, 0) + 16
                ms[i] = ("d", k, dcnt[k])
            elif needed[i]:
                cnt[o["eng"]] += 1
                ms[i] = ("c", o["eng"], cnt[o["eng"]])
        gmax = {}
        for i, o in enumerate(ops):
            if o["dma"] is not None and o.get("grp") is not None:
                k = (o["dma"], o["grp"])
                gmax[k] = max(gmax.get(k, 0), ms[i][2])
        for i, o in enumerate(ops):
            if o["dma"] is not None and o.get("grp") is not None:
                ms[i] = ("d", o["dma"], gmax[(o["dma"], o["grp"])])
        self.ms = ms
        self.final_counts = (dict(cnt), dict(dcnt))
        self.streams = {e: [] for e in ENGINES}
        for i, o in enumerate(ops):
            self.streams[o["eng"]].append(i)
        return sorted(dcnt.keys())

    def emit_engine(self, e, eng, sems, dma_sems):
        ops, deps, ms = self.ops, self.deps, self.ms
        w = {}
        for i in self.streams[e]:
            o = ops[i]
            want = {}
            for j in deps[i]:
                if self._skip(ops[j], e):
                    continue
                m = ms[j]
                key = (m[0], m[1])
                if m[2] > want.get(key, 0):
                    want[key] = m[2]
            for key, val in want.items():
                if w.get(key, 0) >= val:
                    continue
                w[key] = val
                sem = dma_sems[key[1]] if key[0] == "d" else sems[key[1]]
                eng.wait_ge(sem, val)
            ins = o["fn"](eng)
            m = ms[i]
            if m is not None:
                if m[0] == "d":
                    ins.then_inc(dma_sems[m[1]], 16)
                else:
                    ins.then_inc(sems[m[1]], 1)


class Buf:
    def __init__(self, ap, shape, res, base=0, scale=1):
        self.ap, self.shape, self.res, self.base, self.scale = ap, tuple(shape), res, base, scale
        st, s = [], 1
        for d in reversed(self.shape):
            st.append(s)
            s *= d
        self.strides = tuple(reversed(st))
        self.n = s

    def r(self, *ranges):
        lo = hi = 0
        for k, d in enumerate(self.shape):
            a, b = ranges[k] if k < len(ranges) and ranges[k] is not None else (0, d)
            lo += a * self.strides[k]
            hi += (b - 1) * self.strides[k]
        return (self.res, (self.base + lo * self.scale, self.base + (hi + 1) * self.scale))

    def all(self):
        return (self.res, (self.base, self.base + self.n * self.scale))


def _t5_buckets(rel):
    n = -rel
    half = N_BUCKETS // 2
    ret = (n < 0).astype(np.int32) * half
    n = np.abs(n)
    max_exact = half // 2
    large = max_exact + (np.log(np.maximum(n, 1) / max_exact)
                         / np.log(MAX_DISTANCE / max_exact) * (half - max_exact)).astype(np.int32)
    large = np.minimum(large, half - 1)
    return (ret + np.where(n < max_exact, n, large)).astype(np.int32)


def _bias_tiles(rel_bias):
    out = np.full((128, 4, 640), NEG_INF, np.float32)
    i = np.arange(128)[:, None]
    for g, (window, dil) in enumerate(ATTN_GROUPS):
        half = window // (2 * dil)
        buckets = _t5_buckets(dil * np.arange(-half, half + 1))
        for h in range(4):
            col = rel_bias[:, g * 4 + h]
            def fill(dst, delta):
                valid = np.abs(delta) <= half
                j = np.clip(delta + half, 0, 2 * half)
                dst[...] = np.where(valid, col[buckets[j]], np.float32(NEG_INF))
            if g < 2:
                c = np.arange(128)[None, :]
                fill(out[:, h, g * 256:g * 256 + 128], i - c - 64)
                fill(out[:, h, g * 256 + 128:g * 256 + 256], i - c + 64)
            else:
                c = np.arange(32)[None, :]
                for m in range(4):
                    fill(out[:, h, 512 + 32 * m:512 + 32 * m + 32], i - 32 * m - c)
    return out.reshape(128, 2560)


def _band_mats():
    out = np.zeros((128, 20, 128), np.float32)
    for wi, w in enumerate(POOL_WINDOWS):
        def full(t_out0):
            M = np.zeros((3, 128, 128), np.float32)
            for c in range(128):
                t = t_out0 + c
                lo = max(t - w // 2, 0)
                hi = min(t - w // 2 + w, SEQ)
                cnt = hi - lo
                for ti in range(lo, hi):
                    d = ti // 128 - t_out0 // 128
                    M[d + 1, ti % 128, c] += 1.0 / cnt
                M[1, c, c] -= 1.0
            return M
        mi = full(128 * 5)
        out[:, wi * 5 + 0] = mi[0]
        out[:, wi * 5 + 1] = mi[1]
        out[:, wi * 5 + 2] = mi[2]
        out[:, wi * 5 + 3] = full(0)[1]
        out[:, wi * 5 + 4] = full(SEQ - 128)[1]
    return out.reshape(128, 2560)


def _chunked(v, nch):
    return np.ascontiguousarray(v.reshape(nch, 128).T)


def _pack_vecs(norm1, norm2, pool_scale, conv_w, conv_b):
    vecs = np.zeros((128, NV), np.float32)
    for l in range(DEPTH):
        vecs[:, VC_G1 + l * 8:VC_G1 + l * 8 + 8] = _chunked(norm1[l], 8)
        vecs[:, VC_G2 + l * 8:VC_G2 + l * 8 + 8] = _chunked(norm2[l], 8)
        vecs[:, VC_PS + l * 8:VC_PS + l * 8 + 8] = _chunked(pool_scale[l], 8)
        for j in range(3):
            o = VC_CW + (l * 3 + j) * NFC
            vecs[:, o:o + NFC] = _chunked(conv_w[l, j], NFC)
        vecs[:, VC_CB + l * NFC:VC_CB + l * NFC + NFC] = _chunked(conv_b[l], NFC)
    vecs[:, VC_EPS] = EPS
    return vecs


def build_nc(n_layers=DEPTH, final_norm=True, stop=None, dbg=False):
    nc = bass.Bass("TRN2", target_bir_lowering=False)
    dt_in = lambda n, s: nc.dram_tensor(n, s, F32, kind="ExternalInput").ap()
    x_d = dt_in("x", [SEQ, D_MODEL])
    w_in_d = dt_in("w_in", [DEPTH, D_MODEL, IN_WIDTH])
    w_pool_d = dt_in("w_pool", [DEPTH, 4, 256, 256])
    w_a_d = dt_in("w_a", [DEPTH, 1024, 1024])
    w_b_d = dt_in("w_b", [DEPTH, 512, 1024])
    w_o_d = dt_in("w_o", [DEPTH, 1024, 1024])
    w_up_d = dt_in("w_up", [DEPTH, 1024, 2 * D_FF])
    w_down_d = dt_in("w_down", [DEPTH, D_FF, 1024])
    vecs_d = dt_in("vecs", [128, NV])
    normf_d = dt_in("normf_bc", [128, 1024])
    ebraw_d = dt_in("ebraw", [128, 2560])
    bmat_d = dt_in("bmat", [128, 2560])
    y_d = nc.dram_tensor("y", [SEQ, D_MODEL], F32, kind="ExternalOutput").ap()
    dbg_d = {}
    if dbg:
        for l_ in range(n_layers):
            for nm in ("mix", "l"):
                dbg_d["%s%d" % (nm, l_)] = nc.dram_tensor("d_%s%d" % (nm, l_), [128, 8 * SEQ], F32, kind="ExternalOutput").ap()
            dbg_d["attn%d" % l_] = nc.dram_tensor("d_attn%d" % l_, [128, 4 * SEQ], F32, kind="ExternalOutput").ap()

    S = Sched()
    es = ExitStack()
    with es:
        sbt = lambda n, s, d: es.enter_context(nc.sbuf_tensor(n, s, d))
        hT_t = sbt("hT", [128, 8, SEQ], F32)
        xnT_t = sbt("xnT", [128, 8, SEQ], BF16)
        slots_t = [sbt("ws%d" % i, [128, 4096], BF16) for i in range(4)]
        arena_t = sbt("arena", [128, ARENA], BF16)
        ident_t = sbt("ident", [128, 128], F32)
        ones_t = sbt("onesb", [128, 128], BF16)
        invd_t = sbt("invdb", [128, 128], BF16)
        vecs_t = sbt("vecs_sb", [128, NV], F32)
        banks = [es.enter_context(nc.psum_tensor("pb%d" % i, [128, 512], F32)) for i in range(8)]

        hT = Buf(hT_t, (8, SEQ), "hT")
        xnT = Buf(xnT_t, (8, SEQ), "xnT")
        ident = Buf(ident_t, (128,), "ident")
        onesb = Buf(ones_t, (128,), "onesb")
        invdb = Buf(invd_t, (128,), "invdb")
        vecs = Buf(vecs_t, (NV,), "vecs")
        VECS_R = [vecs.all()]

        def vcol(c):
            return vecs_t[:, c:c + 1]

        class Arena:
            def __init__(self, start=0):
                self.off = start

            def take(self, shape, dt=BF16):
                n = int(np.prod(shape))
                ne = n * (2 if dt == F32 else 1)
                self.off = (self.off + 15) // 16 * 16
                assert self.off + ne <= ARENA, ("arena overflow", self.off, ne)
                v = arena_t[:, self.off:self.off + ne]
                if dt == F32:
                    v = v.bitcast(F32)
                if len(shape) == 2:
                    v = v.rearrange("p (a b) -> p a b", a=shape[0])
                elif len(shape) == 3:
                    v = v.rearrange("p (a b c) -> p a b c", a=shape[0], b=shape[1])
                b = Buf(v, shape, "arena", self.off, 2 if dt == F32 else 1)
                self.off += ne
                return b

        gb = [0]
        hb = [0]

        def gbank():
            gb[0] = (gb[0] + 1) % 4
            return gb[0]

        def hbank():
            hb[0] = (hb[0] + 1) % 3
            return 4 + hb[0]

        NORM_BANK = 7

        ab_ = [0]

        def abank():
            ab_[0] = (ab_[0] + 1) % 6
            return ab_[0]

        xb_ = [0]

        def xbank():
            xb_[0] = (xb_[0] + 1) % 7
            return xb_[0]

        def PS(b):
            return ("ps:%d" % b, None)

        def mm(out, lhsT, rhs, start, stop, reads, writes, skip=False):
            kw = dict(skip_group_check=True) if skip else {}
            S.op(PE, lambda e: e.matmul(out, lhsT=lhsT, rhs=rhs, start=start, stop=stop, **kw), reads, writes)

        def tr(out, in_, reads, writes):
            S.op(PE, lambda e: e.transpose(out=out, in_=in_, identity=ident_t[:]), reads + [ident.all()], writes)

        def act(out, in_, func, reads, writes, **kw):
            S.op(ACT, lambda e: e.activation(out=out, in_=in_, func=func, **kw), reads, writes)

        def dcopy(out, in_, reads, writes):
            S.op(DVE, lambda e: e.tensor_copy(out=out, in_=in_), reads, writes)

        tog = [0]

        def evac(out, in_, reads, writes):
            tog[0] ^= 1
            if tog[0]:
                act(out, in_, AF.Copy, reads, writes)
            else:
                dcopy(out, in_, reads, writes)

        def dtt(out, in0, in1, op, reads, writes):
            S.op(DVE, lambda e: e.tensor_tensor(out=out, in0=in0, in1=in1, op=op), reads, writes)

        def dstt(out, in0, scalar, in1, op0, op1, reads, writes):
            S.op(DVE, lambda e: e.scalar_tensor_tensor(out=out, in0=in0, scalar=scalar, in1=in1, op0=op0, op1=op1),
                 reads, writes)

        def drecip(out, in_, reads, writes):
            S.op(DVE, lambda e: e.reciprocal(out=out, in_=in_), reads, writes)

        def ptt(out, in0, in1, op, reads, writes):
            S.op(POOL, lambda e: e.tensor_tensor(out=out, in0=in0, in1=in1, op=op), reads, writes)

        def pts(out, in0, s1, s2, op0, op1, reads, writes):
            S.op(POOL, lambda e: e.tensor_scalar(out=out, in0=in0, scalar1=s1, scalar2=s2, op0=op0, op1=op1),
                 reads, writes)

        def dts(out, in0, s1, s2, op0, op1, reads, writes):
            S.op(DVE, lambda e: e.tensor_scalar(out=out, in0=in0, scalar1=s1, scalar2=s2, op0=op0, op1=op1),
                 reads, writes)

        def dma(q, out, in_, reads, writes, key, grp=None):
            S.op(q, lambda e: e.dma_start(out=out, in_=in_), reads, writes, dma=key, grp=grp)

        slot_i = [0]

        def next_slot():
            slot_i[0] = (slot_i[0] + 1) % 4
            return slot_i[0]

        def SL(i):
            return ("ws%d" % i, None)

        fill_id = [0]

        def new_fill():
            fill_id[0] += 1

        def wload(si, dst, src, part=None):
            box = None if part is None else (part, part + 1)
            dma(POOL, dst, src, [], [("ws%d" % si, box)], "ws%d" % si, grp=(si, fill_id[0]))

        def slot_k8(si, ncols=512):
            return slots_t[si][:, 0:8 * ncols].rearrange("p (k n) -> p k n", k=8)

        S.op(DVE, lambda e: e.memset(ident_t[:], 1.0), [], [ident.all()])
        S.op(POOL, lambda e: e.affine_select(out=ident_t[:], in_=ident_t[:], pattern=[[-1, 128]],
                                               compare_op=ALU.is_equal, fill=0.0, base=0, channel_multiplier=1),
             [ident.all()], [ident.all()])
        S.op(DVE, lambda e: e.memset(ones_t[:], 1.0), [], [onesb.all()])
        S.op(DVE, lambda e: e.memset(invd_t[:], 1.0 / D_MODEL), [], [invdb.all()])
        dma(SP, vecs_t[:], vecs_d, [], VECS_R, "vecs")

        jobs = []

        def job(nslots, loads, compute, name=""):
            jobs.append((nslots, loads, compute, name))

        def input_compute(_):
            ar = Arena()
            xin = [ar.take((1024,), F32) for _ in range(16)]
            for t in range(16):
                dma(SP, xin[t].ap, x_d[128 * t:128 * t + 128, :], [], [xin[t].all()], "xin%d" % t)
            for t in range(16):
                xb = xin[t]
                for hf in range(2):
                    b = xbank()
                    for j in range(4):
                        c = hf * 4 + j
                        tr(banks[b][:, 128 * j:128 * j + 128], xb.ap[:, 128 * c:128 * c + 128], [xb.all()], [PS(b)])
                    evac(hT_t[:, 4 * hf:4 * hf + 4, 128 * t:128 * t + 128],
                         banks[b][:].rearrange("p (a b) -> p a b", a=4),
                         [PS(b)], [hT.r((4 * hf, 4 * hf + 4), (128 * t, 128 * t + 128))])
                if t % 4 == 3 and n_layers > 0:
                    run_tasks(norm_block_tasks(t // 4, VC_G1))

        job(0, None, input_compute, "input")

        _nar = Arena(NORM_OFF)
        n_sq = [_nar.take((512,), BF16) for _ in range(2)]
        n_rs = [_nar.take((512,), F32) for _ in range(2)]

        def norm_block_tasks(blk, gcol, out_t=None, out_buf=None):
            out_t = xnT_t if out_t is None else out_t
            out_buf = xnT if out_buf is None else out_buf
            t0 = 512 * blk
            b = NORM_BANK
            r = n_rs[blk % 2]

            def t_sq(c0):
                def pre():
                    for c in (c0, c0 + 1):
                        s_ = n_sq[c % 2]
                        act(s_.ap, hT_t[:, c, t0:t0 + 512], AF.Square, [hT.r((c, c + 1), (t0, t0 + 512))], [s_.all()])

                def post():
                    for c in (c0, c0 + 1):
                        s_ = n_sq[c % 2]
                        mm(banks[b][:, :], invd_t[:], s_.ap, c == 0, c == 7, [invdb.all(), s_.all()], [PS(b)])
                return (pre, post)

            def t_rstd():
                act(r.ap, banks[b][:, :], AF.Ln, [PS(b)] + VECS_R, [r.all()], bias=vcol(VC_EPS), scale=1.0)
                act(r.ap, r.ap, AF.Exp, [r.all()], [r.all()], scale=-0.5)

            def t_xn(c0):
                def f():
                    for c in (c0, c0 + 1):
                        dstt(out_t[:, c, t0:t0 + 512], hT_t[:, c, t0:t0 + 512], vcol(gcol + c), r.ap, ALU.mult, ALU.mult,
                             [hT.r((c, c + 1), (t0, t0 + 512)), r.all()] + VECS_R,
                             [out_buf.r((c, c + 1), (t0, t0 + 512))])
                return (f, None)

            return [t_sq(0), t_sq(2), t_sq(4), t_sq(6), (None, t_rstd), t_xn(0), t_xn(2), t_xn(4), t_xn(6)]

        def run_tasks(tasks):
            for (pre, post) in tasks:
                if pre is not None:
                    pre()
                if post is not None:
                    post()

        extras = {}

        def spread(tasks, j0, j1):
            n = j1 - j0
            for k, t in enumerate(tasks):
                extras.setdefault(j0 + (k * n) // len(tasks), []).append(t)

        def proj_F(dst_bank, wslot_ap, col0, src_t, src_buf, t0, nk=8, wres=None):
            for kc in range(nk):
                mm(banks[dst_bank][:, :], wslot_ap[:, kc, col0:col0 + 128], src_t[:, kc, t0:t0 + 512],
                   kc == 0, kc == nk - 1,
                   [wres, src_buf.r((kc, kc + 1), (t0, t0 + 512))], [PS(dst_bank)])

        def tiles_of(g):
            res = []
            if g == 0:
                for J in range(17):
                    b0, b1 = max(128 * J - 64, 0), min(128 * J + 64, SEQ)
                    res.append((J, b0 - (128 * J - 64), b1 - b0, b0, 1))
            elif g == 1:
                for r in range(4):
                    for J in range(5):
                        b0, b1 = max(128 * J - 64, 0), min(128 * J + 64, 512)
                        res.append((r * 5 + J, b0 - (128 * J - 64), b1 - b0, r + 4 * b0, 4))
            else:
                for r in range(16):
                    res.append((r, 0, 128, r, 16))
            return res

        def xnf32(c):
            return Buf(xnT_t[:, c, 0:1024].bitcast(F32), (512,), "xnT", base=c * SEQ, scale=2)

        ost_lo, ost_hi = [xnf32(0), xnf32(6)], [xnf32(1), xnf32(7)]
        nf_lo, nf_hi, sqs_lo, sqs_hi = xnf32(2), xnf32(3), xnf32(4), xnf32(5)
        rs2_t = sbt("rs2", [128, 4], F32)
        halo_t = sbt("halo", [128, NFC], F32)
        halo = Buf(halo_t, (NFC,), "halo")
        rs2 = Buf(rs2_t, (4,), "rs2")
        final_done = []

        def final_tile(t):
            if not final_done:
                dma(SP, nf_lo.ap, normf_d[:, 0:512], [], [nf_lo.all()], "nf0")
                dma(SP, nf_hi.ap, normf_d[:, 512:1024], [], [nf_hi.all()], "nf1")
            final_done.append(t)
            ol, oh = ost_lo[t % 2], ost_hi[t % 2]
            bl, bh = xbank(), xbank()
            for c in range(8):
                b = bl if c < 4 else bh
                j = c % 4
                tr(banks[b][:, 128 * j:128 * j + 128], hT_t[:, c, 128 * t:128 * t + 128],
                   [hT.r((c, c + 1), (128 * t, 128 * t + 128))], [PS(b)])
            if final_norm:
                act(sqs_lo.ap, banks[bl][:, :], AF.Square, [PS(bl)], [sqs_lo.all()])
                act(sqs_hi.ap, banks[bh][:, :], AF.Square, [PS(bh)], [sqs_hi.all()])
                S.op(DVE, lambda e: e.reduce_sum(out=rs2_t[:, 0:1], in_=sqs_lo.ap, axis=mybir.AxisListType.X),
                     [sqs_lo.all()], [rs2.r((0, 1))])
                S.op(DVE, lambda e: e.reduce_sum(out=rs2_t[:, 1:2], in_=sqs_hi.ap, axis=mybir.AxisListType.X),
                     [sqs_hi.all()], [rs2.r((1, 2))])
                dtt(rs2_t[:, 2:3], rs2_t[:, 0:1], rs2_t[:, 1:2], ALU.add, [rs2.r((0, 2))], [rs2.r((2, 3))])
                act(rs2_t[:, 3:4], rs2_t[:, 2:3], AF.Ln, [rs2.r((2, 3))] + VECS_R, [rs2.r((3, 4))],
                    bias=vcol(VC_EPS), scale=1.0 / D_MODEL)
                act(rs2_t[:, 3:4], rs2_t[:, 3:4], AF.Exp, [rs2.r((3, 4))], [rs2.r((3, 4))], scale=-0.5)
                dstt(ol.ap, banks[bl][:, :], rs2_t[:, 3:4], nf_lo.ap, ALU.mult, ALU.mult,
                     [PS(bl), rs2.r((3, 4)), nf_lo.all()], [ol.all()])
                dstt(oh.ap, banks[bh][:, :], rs2_t[:, 3:4], nf_hi.ap, ALU.mult, ALU.mult,
                     [PS(bh), rs2.r((3, 4)), nf_hi.all()], [oh.all()])
            else:
                dcopy(ol.ap, banks[bl][:, :], [PS(bl)], [ol.all()])
                dcopy(oh.ap, banks[bh][:, :], [PS(bh)], [oh.all()])
            dma(SP, y_d[128 * t:128 * t + 128, 0:512], ol.ap, [ol.all()], [], "outA%d" % (t % 2))
            dma(SP, y_d[128 * t:128 * t + 128, 512:1024], oh.ap, [oh.all()], [], "outB%d" % (t % 2))

        def add_layer(l):
            w_in_l = w_in_d[l].rearrange("(kc p) n -> p kc n", p=128)
            w_a_l = w_a_d[l].rearrange("(kc p) n -> p kc n", p=128)
            w_b_l = w_b_d[l].rearrange("(kc p) n -> p kc n", p=128)
            w_o_l = w_o_d[l].rearrange("(kc p) n -> p kc n", p=128)
            w_up_l = w_up_d[l].rearrange("(kc p) n -> p kc n", p=128)
            w_dn_l = w_down_d[l].rearrange("(kc p) n -> p kc n", p=128)

            ar = Arena()
            attnT = ar.take((4, SEQ))
            a_mark = ar.off
            ar_n1 = Arena(a_mark)
            ar = Arena(a_mark)
            kT = ar.take((3, SEQ))
            V0 = ar.take((17, 128))
            V1 = ar.take((20, 128))
            V2 = ar.take((16, 128))
            qTb = [ar.take((3, 512)) for _ in range(3)]
            ptr = [ar.take((512,)) for _ in range(5)]
            pt = [ar.take((512,)) for _ in range(5)]
            EB = ar.take((4, 640))
            ebr = ar.take((640,), F32)
            rden = ar.take((512,), F32)
            Vg = (V0, V1, V2)

            def norm1_compute(_):
                for h in range(4):
                    dma(SP, ebr.ap, ebraw_d[:, 640 * h:640 * h + 640], [], [ebr.all()], "ebr")
                    act(EB.ap[:, h, :], ebr.ap, AF.Exp, [ebr.all()], [EB.r((h, h + 1))])

            job(0, None, norm1_compute, "norm1")

            def kv_loads(slots, h):
                for si, off in ((slots[0], OFF_K), (slots[1], OFF_V)):
                    for g in range(3):
                        c0 = off + (g * 4 + h) * 128
                        wload(si, slot_k8(si)[:, :, 128 * g:128 * g + 128], w_in_l[:, :, c0:c0 + 128], part=g)

            def kv_compute(slots, h):
                sk_, sv_ = slots
                wk, wv = slot_k8(sk_), slot_k8(sv_)
                for g in range(3):
                    for blk in range(4):
                        b = xbank()
                        proj_F(b, wk, 128 * g, xnT_t, xnT, 512 * blk, wres=SL(sk_))
                        if g == 0:
                            evac(kT.ap[:, g, 512 * blk:512 * blk + 512], banks[b][:, :], [PS(b)],
                                 [kT.r((g, g + 1), (512 * blk, 512 * blk + 512))])
                        else:
                            d_ = 4 if g == 1 else 16
                            nb_ = 512 // d_
                            evac(kT.ap[:, g, :].rearrange("p (r b) -> p b r", r=d_)[:, nb_ * blk:nb_ * blk + nb_, :],
                                 banks[b][:, :].rearrange("p (b r) -> p b r", r=d_), [PS(b)], [kT.r((g, g + 1))])
                for g in range(3):
                    tl = tiles_of(g)
                    i = 0
                    while i < len(tl):
                        n = 1
                        if tl[i][2] == 128:
                            while (n < 4 and i + n < len(tl) and tl[i + n][2] == 128
                                   and tl[i + n][0] == tl[i][0] + n):
                                n += 1
                        b = xbank()
                        for q_ in range(n):
                            (ti, row0, cnt, tok0, step) = tl[i + q_]
                            for kc in range(8):
                                mm(banks[b][row0:row0 + cnt, 128 * q_:128 * q_ + 128],
                                   xnT_t[:, kc, tok0:tok0 + step * (cnt - 1) + 1:step],
                                   wv[:, kc, 128 * g:128 * g + 128], kc == 0, kc == 7,
                                   [SL(sv_), xnT.r((kc, kc + 1))], [PS(b)])
                        (ti, row0, cnt, tok0, step) = tl[i]
                        if n == 1:
                            evac(Vg[g].ap[row0:row0 + cnt, ti, :], banks[b][row0:row0 + cnt, 0:128], [PS(b)],
                                 [Vg[g].r((ti, ti + 1))])
                        else:
                            evac(Vg[g].ap[:, ti:ti + n, :],
                                 banks[b][:, 0:128 * n].rearrange("p (a b) -> p a b", a=n), [PS(b)],
                                 [Vg[g].r((ti, ti + n))])
                        i += n

            def q_loads(slots, h):
                si = slots[0]
                for g in range(3):
                    c0 = OFF_Q + (g * 4 + h) * 128
                    wload(si, slot_k8(si)[:, :, 128 * g:128 * g + 128], w_in_l[:, :, c0:c0 + 128], part=g)

            def q_compute(slots, h):
                sq_ = slots[0]
                wq = slot_k8(sq_)
                base_tag = S.tag

                def qproj(m):
                    S.tag = base_tag + ".qproj%d" % m
                    qb = qTb[m % 3]
                    for g in range(3):
                        b = gbank()
                        proj_F(b, wq, 128 * g, xnT_t, xnT, 512 * m, wres=SL(sq_))
                        if g == 0:
                            evac(qb.ap[:, g, :], banks[b][:, :], [PS(b)], [qb.r((g, g + 1))])
                        else:
                            d_ = 4 if g == 1 else 16
                            evac(qb.ap[:, g, :].rearrange("p (r c) -> p c r", r=d_),
                                 banks[b][:, :].rearrange("p (c r) -> p c r", r=d_), [PS(b)], [qb.r((g, g + 1))])

                def block_groups(m):
                    qb = qTb[m % 3]
                    nb, db = (4, 5) if m % 2 == 0 else (6, 7)
                    groups = []
                    for up in range(2):
                        items = []
                        for s_ in range(4):
                            J = 4 * m + s_ + up
                            b0, b1 = max(128 * J - 64, 0), min(128 * J + 64, SEQ)
                            row0, cnt = b0 - (128 * J - 64), b1 - b0
                            items.append((128 * s_, row0, cnt, kT.ap[:, 0, b0:b1], qb.ap[:, 0, 128 * s_:128 * s_ + 128],
                                          V0.ap[row0:row0 + cnt, J, :],
                                          banks[nb][:, 128 * s_:128 * s_ + 128], banks[db][:, 128 * s_:128 * s_ + 128]))
                        groups.append((items, (128 * up, 128 * up + 128), 4, 128))
                    for up in range(2):
                        items = []
                        for r in range(4):
                            J = m + up
                            b0, b1 = max(128 * J - 64, 0), min(128 * J + 64, 512)
                            row0, cnt = b0 - (128 * J - 64), b1 - b0
                            items.append((128 * r, row0, cnt,
                                          kT.ap[:, 1, 512 * r + b0:512 * r + b1],
                                          qb.ap[:, 1, 128 * r:128 * r + 128],
                                          V1.ap[row0:row0 + cnt, r * 5 + J, :],
                                          banks[nb][:, r:512:4], banks[db][:, r:512:4]))
                        groups.append((items, (256 + 128 * up, 256 + 128 * up + 128), 4, 128))
                    items = []
                    for r in range(16):
                        items.append((32 * r, 0, 128, kT.ap[:, 2, 128 * r:128 * r + 128], qb.ap[:, 2, 32 * r:32 * r + 32],
                                      V2.ap[:, r, :], banks[nb][:, r:512:16], banks[db][:, r:512:16]))
                    groups.append((items, (512 + 32 * m, 512 + 32 * m + 32), 16, 32))
                    return groups

                seq = []
                for m in range(4):
                    for gi, G in enumerate(block_groups(m)):
                        seq.append((m, gi, G))
                pidx = [0]

                def scores(n):
                    m, gi, (items, ebcols, nrep, ncol) = seq[n]
                    S.tag = base_tag + ".m%d.sc%d" % (m, gi)
                    qb = qTb[m % 3]
                    b = gbank()
                    for (c0, row0, cnt, lT, rh, vap, no, do) in items:
                        mm(banks[b][row0:row0 + cnt, c0:c0 + ncol], lT, rh, True, True,
                           [kT.all(), qb.all()], [PS(b)])
                    k = n % 5
                    act(ptr[k].ap, banks[b][:, :], AF.Exp, [PS(b)], [ptr[k].all()], scale=float(128 ** -0.5))
                    ebv = EB.ap[:, h, ebcols[0]:ebcols[1]].unsqueeze(1).broadcast_to([128, nrep, ncol])
                    dtt(pt[k].ap.rearrange("p (a b) -> p a b", a=nrep),
                        ptr[k].ap.rearrange("p (a b) -> p a b", a=nrep), ebv, ALU.mult,
                        [ptr[k].all(), EB.r((h, h + 1))], [pt[k].all()])

                def pv(n):
                    m, gi, (items, ebcols, nrep, ncol) = seq[n]
                    S.tag = base_tag + ".m%d.pv%d" % (m, gi)
                    nb, db = (4, 5) if m % 2 == 0 else (6, 7)
                    k = n % 5
                    first = (gi == 0)
                    for (c0, row0, cnt, lT, rh, vap, no, do) in items:
                        mm(no, vap, pt[k].ap[row0:row0 + cnt, c0:c0 + ncol], first, False,
                           [Vg[0].all(), Vg[1].all(), Vg[2].all(), pt[k].all()], [PS(nb)], skip=True)
                        mm(do, ones_t[row0:row0 + cnt, :], pt[k].ap[row0:row0 + cnt, c0:c0 + ncol], first, False,
                           [onesb.all(), pt[k].all()], [PS(db)], skip=True)
                        first = False
                    if gi == 4:
                        S.tag = base_tag + ".m%d.norm" % m
                        act(rden.ap, banks[db][:, :], AF.Ln, [PS(db)], [rden.all()])
                        act(rden.ap, rden.ap, AF.Exp, [rden.all()], [rden.all()], scale=-1.0)
                        dtt(attnT.ap[:, h, 512 * m:512 * m + 512], banks[nb][:, :], rden.ap, ALU.mult,
                            [PS(nb), rden.all()], [attnT.r((h, h + 1), (512 * m, 512 * m + 512))])

                LOOK = 4
                qproj(0)
                qproj(1)
                for n in range(LOOK):
                    scores(n)
                qproj(2)
                for n in range(len(seq)):
                    m, gi, _ = seq[n]
                    if gi == 1 and m == 1:
                        qproj(3)
                    pv(n)
                    if n + LOOK < len(seq):
                        scores(n + LOOK)

            for h in range(4):
                job(2, (lambda s, h=h: kv_loads(s, h)), (lambda s, h=h: kv_compute(s, h)), "kv%d" % h)
                job(1, (lambda s, h=h: q_loads(s, h)), (lambda s, h=h: q_compute(s, h)), "q%d" % h)

            ar = Arena(a_mark)
            bmat = ar.take((20, 128))
            u_sb = ar.take((9, 256))
            pooledT = ar.take((2, 1024))
            pool_outT = ar.take((8, 1024))
            merged = ar.take((8, 1024))
            sgp = [ar.take((512,)) for _ in range(2)]
            sga = [ar.take((512,)) for _ in range(2)]
            t1 = ar.take((512,), F32)
            t2 = ar.take((512,), F32)

            def pool_loads(slots, g):
                si = slots[0]
                wload(si, slots_t[si][:, 0:2048].rearrange("p (k n) -> p k n", k=8), w_in_l[:, :, 256 * g:256 * g + 256],
                      part=0)
                wload(si, slots_t[si][:, 2048:2560].rearrange("p (k n) -> p k n", k=2),
                      w_pool_d[l, g].rearrange("(kc p) n -> p kc n", p=128), part=1)

            def pool_compute(slots, hf, g):
                si = slots[0]
                wu = slots_t[si][:, 0:2048].rearrange("p (k n) -> p k n", k=8)
                wp = slots_t[si][:, 2048:2560].rearrange("p (k n) -> p k n", k=2)
                tl0 = max(8 * hf - 1, 0)
                tl1 = min(8 * hf + 9, 16)
                if hf == 0 and g == 0:
                    if dbg:
                        dma(POOL, dbg_d["attn%d" % l], attnT.ap.rearrange("p a b -> p (a b)"), [attnT.all()], [], "dbg")
                    dma(POOL, bmat.ap.rearrange("p a b -> p (a b)"), bmat_d, [], [bmat.all()], "bmat")
                for t in range(tl0, tl1):
                    b = xbank()
                    for kc in range(8):
                        mm(banks[b][:, 0:256], xnT_t[:, kc, 128 * t:128 * t + 128], wu[:, kc, :], kc == 0, kc == 7,
                           [SL(si), xnT.r((kc, kc + 1), (128 * t, 128 * t + 128))], [PS(b)])
                    evac(u_sb.ap[:, t - tl0, :], banks[b][:, 0:256], [PS(b)], [u_sb.r((t - tl0, t - tl0 + 1))])
                for cc in range(2):
                    for tb in range(2):
                        b = xbank()
                        for j in range(4):
                            t = 8 * hf + 4 * tb + j
                            terms = []
                            if t > 0:
                                terms.append((t - 1, 5 * g + 0))
                            terms.append((t, 5 * g + (3 if t == 0 else 4 if t == 15 else 1)))
                            if t < 15:
                                terms.append((t + 1, 5 * g + 2))
                            for k, (tin, bi) in enumerate(terms):
                                mm(banks[b][:, 128 * j:128 * j + 128],
                                   u_sb.ap[:, tin - tl0, 128 * cc:128 * cc + 128], bmat.ap[:, bi, :],
                                   k == 0, k == len(terms) - 1,
                                   [u_sb.r((tin - tl0, tin - tl0 + 1)), bmat.all()], [PS(b)])
                        evac(pooledT.ap[:, cc, 512 * tb:512 * tb + 512], banks[b][:, :], [PS(b)],
                             [pooledT.r((cc, cc + 1), (512 * tb, 512 * tb + 512))])
                for dc in range(2):
                    for tb in range(2):
                        b = xbank()
                        for kc in range(2):
                            mm(banks[b][:, :], wp[:, kc, 128 * dc:128 * dc + 128],
                               pooledT.ap[:, kc, 512 * tb:512 * tb + 512], kc == 0, kc == 1,
                               [SL(si), pooledT.r((kc, kc + 1), (512 * tb, 512 * tb + 512))], [PS(b)])
                        ch = 2 * g + dc
                        act(pool_outT.ap[:, ch, 512 * tb:512 * tb + 512], banks[b][:, :], AF.Identity,
                            [PS(b)] + VECS_R, [pool_outT.r((ch, ch + 1), (512 * tb, 512 * tb + 512))],
                            scale=vcol(VC_PS + 8 * l + ch))

            def mview(si, part):
                if part < 3:
                    return slots_t[si][:, 1024 * part:1024 * part + 1024].rearrange("p (k n) -> p k n", k=8)
                return slots_t[si][:, 3072:3584].rearrange("p (k n) -> p k n", k=4)

            def merge_loads(slots, mch):
                si = slots[0]
                wload(si, mview(si, 0), w_in_l[:, :, OFF_GP + 128 * mch:OFF_GP + 128 * mch + 128], part=0)
                wload(si, mview(si, 1), w_in_l[:, :, OFF_GA + 128 * mch:OFF_GA + 128 * mch + 128], part=1)
                wload(si, mview(si, 2), w_a_l[:, :, 128 * mch:128 * mch + 128], part=2)
                wload(si, mview(si, 3), w_b_l[:, :, 128 * mch:128 * mch + 128], part=3)

            def merge_compute(slots, hf, mch):
                si = slots[0]
                T0 = 1024 * hf
                wgp, wga, wa, wb = mview(si, 0), mview(si, 1), mview(si, 2), mview(si, 3)
                for tb in range(2):
                    t0 = T0 + 512 * tb
                    gp, ga = sgp[tb], sga[tb]
                    b = gbank()
                    proj_F(b, wgp, 0, xnT_t, xnT, t0, wres=SL(si))
                    act(gp.ap, banks[b][:, :], AF.Sigmoid, [PS(b)], [gp.all()])
                    b = gbank()
                    proj_F(b, wga, 0, xnT_t, xnT, t0, wres=SL(si))
                    act(ga.ap, banks[b][:, :], AF.Sigmoid, [PS(b)], [ga.all()])
                    b = hbank()
                    for kc in range(8):
                        mm(banks[b][:, :], wa[:, kc, :], pool_outT.ap[:, kc, 512 * tb:512 * tb + 512], kc == 0, kc == 7,
                           [SL(si), pool_outT.r((kc, kc + 1), (512 * tb, 512 * tb + 512))], [PS(b)])
                    dtt(t1.ap, banks[b][:, :], gp.ap, ALU.mult, [PS(b), gp.all()], [t1.all()])
                    b = hbank()
                    for kc in range(4):
                        mm(banks[b][:, :], wb[:, kc, :], attnT.ap[:, kc, t0:t0 + 512], kc == 0, kc == 3,
                           [SL(si), attnT.r((kc, kc + 1), (t0, t0 + 512))], [PS(b)])
                    dtt(t2.ap, banks[b][:, :], ga.ap, ALU.mult, [PS(b), ga.all()], [t2.all()])
                    ptt(merged.ap[:, mch, 512 * tb:512 * tb + 512], t1.ap, t2.ap, ALU.add,
                        [t1.all(), t2.all()], [merged.r((mch, mch + 1), (512 * tb, 512 * tb + 512))])

            def wo_loads(slots, mq):
                si = slots[0]
                wload(si, slot_k8(si), w_o_l[:, :, 512 * mq:512 * mq + 512])

            def wo_compute(slots, hf, mq):
                so = slots[0]
                T0 = 1024 * hf
                for mi in range(4):
                    mch = 4 * mq + mi
                    for tb in range(2):
                        t0 = T0 + 512 * tb
                        b = xbank()
                        for kc in range(8):
                            mm(banks[b][:, :], slot_k8(so)[:, kc, 128 * mi:128 * mi + 128],
                               merged.ap[:, kc, 512 * tb:512 * tb + 512], kc == 0, kc == 7,
                               [SL(so), merged.r((kc, kc + 1), (512 * tb, 512 * tb + 512))], [PS(b)])
                        dtt(hT_t[:, mch, t0:t0 + 512], banks[b][:, :], hT_t[:, mch, t0:t0 + 512], ALU.add,
                            [PS(b), hT.r((mch, mch + 1), (t0, t0 + 512))], [hT.r((mch, mch + 1), (t0, t0 + 512))])

            def wo1_loads(slots):
                for mq in range(2):
                    wload(slots[mq], slot_k8(slots[mq]), w_o_l[:, :, 512 * mq:512 * mq + 512])

            def wo1_compute(slots):
                tasks = norm_block_tasks(2, VC_G2 + 8 * l)
                for tb in range(2):
                    t0 = 1024 + 512 * tb
                    for mch in range(8):
                        so, mi = slots[mch // 4], mch % 4
                        tk = tasks[mch] if tb == 1 else (None, None)
                        if tk[0] is not None:
                            tk[0]()
                        b = xbank()
                        for kc in range(8):
                            mm(banks[b][:, :], slot_k8(so)[:, kc, 128 * mi:128 * mi + 128],
                               merged.ap[:, kc, 512 * tb:512 * tb + 512], kc == 0, kc == 7,
                               [SL(so), merged.r((kc, kc + 1), (512 * tb, 512 * tb + 512))], [PS(b)])
                        dtt(hT_t[:, mch, t0:t0 + 512], banks[b][:, :], hT_t[:, mch, t0:t0 + 512], ALU.add,
                            [PS(b), hT.r((mch, mch + 1), (t0, t0 + 512))], [hT.r((mch, mch + 1), (t0, t0 + 512))])
                        if tk[1] is not None:
                            tk[1]()
                run_tasks(tasks[8:])

            for hf in range(2):
                if hf == 1:
                    jp = len(jobs)
                    spread(norm_block_tasks(0, VC_G2 + 8 * l), jp, jp + 9)
                    spread(norm_block_tasks(1, VC_G2 + 8 * l), jp + 9, jp + 13)
                for g in range(4):
                    job(1, (lambda s, g=g: pool_loads(s, g)), (lambda s, hf=hf, g=g: pool_compute(s, hf, g)), "pool")
                for mch in range(8):
                    job(1, (lambda s, mch=mch: merge_loads(s, mch)), (lambda s, hf=hf, mch=mch: merge_compute(s, hf, mch)), "merge")
                if hf == 0:
                    for mq in range(2):
                        job(1, (lambda s, mq=mq: wo_loads(s, mq)), (lambda s, hf=hf, mq=mq: wo_compute(s, hf, mq)), "wo")
                else:
                    job(2, wo1_loads, wo1_compute, "wo1")

            def after_mix(_):
                if dbg:
                    dma(SP, dbg_d["mix%d" % l], hT_t[:].rearrange("p a b -> p (a b)"), [hT.all()], [], "dbg")

            job(0, None, after_mix)
            if stop == 'mix':
                return

            ar = Arena()
            actT = ar.take((NFC, 1024))
            a_sb = [ar.take((1040,), F32) for _ in range(2)]
            cv = [ar.take((1024,), F32) for _ in range(2)]
            gel = [ar.take((1024,)) for _ in range(2)]
            ar_n2 = ar
            cwc = lambda j, c: vcol(VC_CW + (l * 3 + j) * NFC + c)

            j_up0 = len(jobs)
            spread(norm_block_tasks(3, VC_G2 + 8 * l), j_up0, j_up0 + 14)

            def up_loads(slots, fq):
                nch = 4 if fq < 5 else 2
                sa_s, sg_s = slots
                wload(sa_s, slot_k8(sa_s)[:, :, 0:128 * nch], w_up_l[:, :, 512 * fq:512 * fq + 128 * nch])
                wload(sg_s, slot_k8(sg_s)[:, :, 0:128 * nch], w_up_l[:, :, D_FF + 512 * fq:D_FF + 512 * fq + 128 * nch])

            def up_compute(slots, hf, fq):
                nch = 4 if fq < 5 else 2
                sa_s, sg_s = slots
                T0 = 1024 * hf

                def part_a(ci):
                    fc = 4 * fq + ci
                    ab = a_sb[fc % 2]
                    cb_ = cv[fc % 2]
                    gl = gel[fc % 2]
                    for tb in range(2):
                        b = gbank()
                        proj_F(b, slot_k8(sa_s)[:, :, 128 * ci:128 * ci + 128], 0, xnT_t, xnT, T0 + 512 * tb, wres=SL(sa_s))
                        act(ab.ap[:, 8 + 512 * tb:8 + 512 * tb + 512], banks[b][:, :], AF.Copy, [PS(b)],
                            [ab.r((8 + 512 * tb, 8 + 512 * tb + 512))])
                def part_a2(ci):
                    fc = 4 * fq + ci
                    ab = a_sb[fc % 2]
                    cb_ = cv[fc % 2]
                    gl = gel[fc % 2]
                    if hf == 0:
                        dcopy(halo_t[:, fc:fc + 1], ab.ap[:, 1031:1032], [ab.r((1031, 1032))], [halo.r((fc, fc + 1))])
                    for side, tok, col in ((0, T0 - 1, 7), (1, T0 + 1024, 8 + 1024)):
                        if tok < 0 or tok >= SEQ:
                            S.op(POOL, lambda e, ab=ab, col=col: e.memset(ab.ap[:, col:col + 1], 0.0), [],
                                 [ab.r((col, col + 1))])
                        elif side == 0:
                            dcopy(ab.ap[:, col:col + 1], halo_t[:, fc:fc + 1], [halo.r((fc, fc + 1))], [ab.r((col, col + 1))])
                        else:
                            b = gbank()
                            for kc in range(8):
                                mm(banks[b][:, 0:1], slot_k8(sa_s)[:, kc, 128 * ci:128 * ci + 128],
                                   xnT_t[:, kc, tok:tok + 1], kc == 0, kc == 7,
                                   [SL(sa_s), xnT.r((kc, kc + 1), (tok, tok + 1))], [PS(b)])
                            dcopy(ab.ap[:, col:col + 1], banks[b][:, 0:1], [PS(b)], [ab.r((col, col + 1))])
                    dts(cb_.ap, ab.ap[:, 8:8 + 1024], cwc(1, fc), vcol(VC_CB + l * NFC + fc), ALU.mult, ALU.add,
                        [ab.r((8, 1032))] + VECS_R, [cb_.all()])
                    dstt(cb_.ap, ab.ap[:, 7:7 + 1024], cwc(0, fc), cb_.ap, ALU.mult, ALU.add,
                         [ab.r((7, 1031)), cb_.all()] + VECS_R, [cb_.all()])
                    dstt(cb_.ap, ab.ap[:, 9:9 + 1024], cwc(2, fc), cb_.ap, ALU.mult, ALU.add,
                         [ab.r((9, 1033)), cb_.all()] + VECS_R, [cb_.all()])
                    act(gl.ap, cb_.ap, AF.Gelu_apprx_tanh, [cb_.all()], [gl.all()])

                def part_g(ci):
                    fc = 4 * fq + ci
                    gl = gel[fc % 2]
                    for tb in range(2):
                        b = hbank()
                        proj_F(b, slot_k8(sg_s)[:, :, 128 * ci:128 * ci + 128], 0, xnT_t, xnT, T0 + 512 * tb, wres=SL(sg_s))
                        dtt(actT.ap[:, fc, 512 * tb:512 * tb + 512], banks[b][:, :],
                            gl.ap[:, 512 * tb:512 * tb + 512], ALU.mult,
                            [PS(b), gl.r((512 * tb, 512 * tb + 512))],
                            [actT.r((fc, fc + 1), (512 * tb, 512 * tb + 512))])

                part_a(0)
                part_a2(0)
                for ci in range(nch):
                    if ci + 1 < nch:
                        part_a(ci + 1)
                    part_g(ci)
                    if ci + 1 < nch:
                        part_a2(ci + 1)

            def dn_view(sd):
                return slots_t[sd][:, 0:NFC * 128].rearrange("p (k n) -> p k n", k=NFC)

            def dn_loads(slots, mch):
                wload(slots[0], dn_view(slots[0]), w_dn_l[:, :, 128 * mch:128 * mch + 128])

            def dn_compute(slots, hf, mch):
                sd = slots[0]
                T0 = 1024 * hf
                wd_v = dn_view(sd)
                for tb in range(2):
                    t0 = T0 + 512 * tb
                    b = xbank()
                    for kc in range(NFC):
                        mm(banks[b][:, :], wd_v[:, kc, :], actT.ap[:, kc, 512 * tb:512 * tb + 512],
                           kc == 0, kc == NFC - 1,
                           [SL(sd), actT.r((kc, kc + 1), (512 * tb, 512 * tb + 512))], [PS(b)])
                    dtt(hT_t[:, mch, t0:t0 + 512], banks[b][:, :], hT_t[:, mch, t0:t0 + 512], ALU.add,
                        [PS(b), hT.r((mch, mch + 1), (t0, t0 + 512))], [hT.r((mch, mch + 1), (t0, t0 + 512))])

            ftile = [0]

            def with_final(fn, hf, is_dn=False):
                if is_dn and hf == 1 and l == n_layers - 1 and ftile[0] < 8:
                    t = ftile[0]
                    ftile[0] += 1
                    return lambda s: (final_tile(t), fn(s))
                return fn

            for hf in range(2):
                for fq in range(6):
                    job(2, (lambda s, fq=fq: up_loads(s, fq)),
                        with_final((lambda s, hf=hf, fq=fq: up_compute(s, hf, fq)), hf), "up")
                for mch in range(8):
                    job(1, (lambda s, mch=mch: dn_loads(s, mch)),
                        with_final((lambda s, hf=hf, mch=mch: dn_compute(s, hf, mch)), hf, True), "dn")

            if l + 1 < n_layers:
                spread(norm_block_tasks(0, VC_G1 + 8 * (l + 1)) + norm_block_tasks(1, VC_G1 + 8 * (l + 1)),
                       len(jobs) - 14, len(jobs))
                job(0, None, lambda _: run_tasks(norm_block_tasks(2, VC_G1 + 8 * (l + 1))
                                                  + norm_block_tasks(3, VC_G1 + 8 * (l + 1))), "norm1b23")

            def after_layer(_):
                if dbg:
                    dma(SP, dbg_d["l%d" % l], hT_t[:].rearrange("p a b -> p (a b)"), [hT.all()], [], "dbg")

            job(0, None, after_layer)

        for l in range(n_layers):
            add_layer(l)

        assigned = []
        for (ns, lo, co, nm) in jobs:
            assigned.append([next_slot() for _ in range(ns)])
        loaded = -1
        for i, (ns, lo, co, nm) in enumerate(jobs):
            while loaded + 1 < len(jobs) and sum(jobs[k][0] for k in range(i, loaded + 2)) <= 4:
                loaded += 1
                if jobs[loaded][1] is not None:
                    new_fill()
                    jobs[loaded][1](assigned[loaded])
            if loaded < i:
                loaded = i
                if lo is not None:
                    new_fill()
                    lo(assigned[i])
            ex = extras.get(i, [])
            S.tag = "job%d:%s.x" % (i, nm)
            if ex and ex[0][0] is not None:
                ex[0][0]()
            S.tag = "job%d:%s" % (i, nm)
            co(assigned[i])
            S.tag = "job%d:%s.y" % (i, nm)
            if ex and ex[0][1] is not None:
                ex[0][1]()
            run_tasks(ex[1:])
            S.tag = ""

        for t in range(16):
            if t not in final_done:
                final_tile(t)

        dkeys = S.prepare()
        global _MS_TABLE
        _MS_TABLE = {(m[1], m[2]): S.ops[i]["tag"] for i, m in enumerate(S.ms) if m is not None}
        sems = {e: es.enter_context(nc.semaphore("s_" + e)) for e in ENGINES}
        dsems = {k: es.enter_context(nc.semaphore("d_" + k)) for k in dkeys}
        block = es.enter_context(nc.Block())

        @block.sync
        def _(eng):
            S.emit_engine(SP, eng, sems, dsems)
            for k in ("outA0", "outA1", "outB0", "outB1") + (("dbg",) if dbg else ()):
                eng.wait_ge(dsems[k], S.final_counts[1][k])

        @block.scalar
        def _(eng):
            S.emit_engine(ACT, eng, sems, dsems)

        @block.vector
        def _(eng):
            S.emit_engine(DVE, eng, sems, dsems)

        @block.gpsimd
        def _(eng):
            S.emit_engine(POOL, eng, sems, dsems)

        @block.tensor
        def _(eng):
            S.emit_engine(PE, eng, sems, dsems)
    print('[kernel] ops', len(S.ops), S.final_counts, flush=True)
    return nc


_NC_CACHE = {}
_MS_TABLE = {}


def _run(x, shared, n_layers=DEPTH, final_norm=True, stop=None, dbg=False):
    key = (n_layers, final_norm, stop, dbg)
    if key not in _NC_CACHE:
        _NC_CACHE[key] = build_nc(n_layers, final_norm, stop, dbg)
    nc = _NC_CACHE[key]
    in_maps = [dict(shared, x=np.ascontiguousarray(x[b])) for b in range(8)]
    res = run_bass_kernel_spmd(nc, in_maps, core_ids=list(range(8)))
    if dbg:
        return np.stack([r["y"] for r in res.results], axis=0), res.results[0]
    return np.stack([r["y"] for r in res.results], axis=0)


def kernel(x, w_in, w_pool, pool_scale, w_a, w_b, w_o, norm1, norm2,
           w_up, conv_w, conv_b, w_down, rel_bias, norm_f):
    f = lambda a: np.ascontiguousarray(np.asarray(a, dtype=np.float32))
    shared = dict(
        w_in=f(w_in), w_pool=f(w_pool), w_a=f(w_a), w_b=f(w_b), w_o=f(w_o), w_up=f(w_up), w_down=f(w_down),
        vecs=_pack_vecs(f(norm1), f(norm2), f(pool_scale), f(conv_w), f(conv_b)),
        normf_bc=np.ascontiguousarray(np.broadcast_to(f(norm_f)[None, :], (128, 1024))),
        ebraw=_bias_tiles(f(rel_bias)),
        bmat=_band_mats(),
    )
    return _run(f(x), shared).astype(np.float32)
```
